# Optimizing a Trainium2 kernel written in Bass

```python
import jax, jax.numpy as jnp
from jax import lax
import numpy as np

D_MODEL = 1024
BATCH = 8
SEQ = 4096
DEPTH = 1

N_META = 16
D_CONV = D_MODEL
CONV_WIDTH = 31
HG_HEADS = 8
HG_DK = 128
HG_DV = D_MODEL // HG_HEADS
D_HK = HG_HEADS * HG_DK
D_HV = HG_HEADS * HG_DV
CHUNK = 64
EPS = 1e-6
SPLIT_SIZES = (D_CONV, D_CONV, D_CONV, D_HK, D_HK, D_HV, D_HV, D_MODEL, D_MODEL)
D_IN = D_CONV * 3 + D_HK * 2 + D_HV * 2 + D_MODEL * 2

kernel_name = "hybrid_conformer_hgrn2_gated_block"


def rmsnorm(x, g):
    xf = x.astype(jnp.float32)
    y = xf * lax.rsqrt(jnp.mean(xf * xf, axis=-1, keepdims=True) + EPS)
    return (y * g.astype(jnp.float32)).astype(x.dtype)


def layernorm(x, g, b):
    xf = x.astype(jnp.float32)
    mu = jnp.mean(xf, axis=-1, keepdims=True)
    var = jnp.mean(jnp.square(xf - mu), axis=-1, keepdims=True)
    y = (xf - mu) * lax.rsqrt(var + EPS)
    return (y * g.astype(jnp.float32) + b.astype(jnp.float32)).astype(x.dtype)


def conformer_branch(u_a, u_b, z, conv_w, conv_b, ln_g, ln_b, w_out):
    a = u_a * jax.nn.sigmoid(u_b)
    c = lax.conv_general_dilated(
        a, conv_w[:, None, :].astype(a.dtype), window_strides=(1,),
        padding=[(CONV_WIDTH - 1, 0)],
        dimension_numbers=('NWC', 'WIO', 'NWC'),
        feature_group_count=D_CONV) + conv_b
    c = jax.nn.silu(layernorm(c, ln_g, ln_b))
    return (c * jax.nn.silu(z)) @ w_out


def hgrn2_branch(q_raw, f_raw, i_raw, g, lb, gnorm_g, w_out):
    bsz, seqlen, _ = q_raw.shape
    out_dtype = i_raw.dtype
    q = jax.nn.silu(q_raw.astype(jnp.float32))
    f = lb + (1.0 - lb) * jax.nn.sigmoid(f_raw.astype(jnp.float32))
    log_f = jnp.log(f)
    k = 1.0 - f
    v = i_raw.astype(jnp.float32)
    pad = (-seqlen) % CHUNK
    n_chunks = (seqlen + pad) // CHUNK

    def to_chunks(t, d):
        t = jnp.pad(t, ((0, 0), (pad, 0), (0, 0)))
        t = t.reshape(bsz, n_chunks, CHUNK, HG_HEADS, d)
        return jnp.transpose(t, (1, 0, 3, 2, 4))

    qc = to_chunks(q, HG_DK)
    kc = to_chunks(k, HG_DK)
    vc = to_chunks(v, HG_DV)
    bc = jnp.cumsum(to_chunks(log_f, HG_DK), axis=3)
    causal = jnp.tril(jnp.ones((CHUNK, CHUNK), dtype=bool))[None, None, :, :, None]

    def step(S, inp):
        qi, ki, vi, bi = inp
        o_inter = jnp.einsum('bhtk,bhkv->bhtv', qi * jnp.exp(bi), S)
        diff = bi[:, :, :, None, :] - bi[:, :, None, :, :]
        decay = jnp.exp(jnp.where(causal, diff, -jnp.inf))
        attn = jnp.einsum('bhtk,bhsk,bhtsk->bhts', qi, ki, decay)
        o_intra = jnp.einsum('bhts,bhsv->bhtv', attn, vi)
        b_last = bi[:, :, -1:, :]
        S_new = jnp.exp(b_last[:, :, 0, :])[..., None] * S + jnp.einsum(
            'bhsk,bhsv->bhkv', ki * jnp.exp(b_last - bi), vi)
        return S_new, o_inter + o_intra

    S0 = jnp.zeros((bsz, HG_HEADS, HG_DK, HG_DV), jnp.float32)
    _, o = lax.scan(step, S0, (qc, kc, vc, bc))
    o = jnp.transpose(o, (1, 0, 3, 2, 4)).reshape(bsz, n_chunks * CHUNK, HG_HEADS, HG_DV)
    o = o[:, pad:]
    o = o * lax.rsqrt(jnp.mean(o * o, axis=-1, keepdims=True) + EPS)
    o = o * gnorm_g.astype(jnp.float32).reshape(HG_HEADS, HG_DV)
    o = o.reshape(bsz, seqlen, D_HV).astype(out_dtype)
    return (o * jax.nn.silu(g)) @ w_out


def setup_inputs(seed: int = 0) -> dict:
    key = jax.random.key(seed)
    ks = jax.random.split(key, 16)
    f32 = jnp.float32
    nrm = lambda k, shape, s: jax.random.normal(k, shape, f32) * s
    return {
        "x": nrm(ks[0], (BATCH, SEQ, D_MODEL), 1.0),
        "meta_tokens": nrm(ks[1], (N_META, D_MODEL), 1.0),
        "norm_g": 1.0 + nrm(ks[2], (DEPTH, D_MODEL), 0.02),
        "w_in": nrm(ks[3], (DEPTH, D_MODEL, D_IN), D_MODEL ** -0.5),
        "conv_w": nrm(ks[4], (DEPTH, CONV_WIDTH, D_CONV), CONV_WIDTH ** -0.5),
        "conv_b": nrm(ks[5], (DEPTH, D_CONV), 0.02),
        "ln_g": 1.0 + nrm(ks[6], (DEPTH, D_CONV), 0.02),
        "ln_b": nrm(ks[7], (DEPTH, D_CONV), 0.02),
        "w_conv_out": nrm(ks[8], (DEPTH, D_CONV, D_MODEL), D_CONV ** -0.5),
        "lb_logits": nrm(ks[9], (DEPTH + 1, D_HK), 0.5),
        "gnorm_g": 1.0 + nrm(ks[10], (DEPTH, D_HV), 0.02),
        "w_rec_out": nrm(ks[11], (DEPTH, D_HV, D_MODEL), D_HV ** -0.5),
        "w_out": nrm(ks[12], (DEPTH, D_MODEL, D_MODEL), D_MODEL ** -0.5),
        "final_g": 1.0 + nrm(ks[13], (D_MODEL,), 0.02),
    }


def reference(x, meta_tokens, norm_g, w_in, conv_w, conv_b, ln_g, ln_b, w_conv_out,
              lb_logits, gnorm_g, w_rec_out, w_out, final_g):
    bsz = x.shape[0]
    meta = jnp.broadcast_to(meta_tokens.astype(x.dtype)[None], (bsz, N_META, D_MODEL))
    h_res = jnp.concatenate([meta, x], axis=1)
    lb_all = jnp.cumsum(jax.nn.softmax(lb_logits.astype(jnp.float32), axis=0), axis=0)
    split_idx = [int(v) for v in np.cumsum(SPLIT_SIZES)[:-1]]
    for l in range(DEPTH):
        h = rmsnorm(h_res, norm_g[l])
        proj = h @ w_in[l]
        glu_a, glu_b, z_conv, q, f, i, g_rec, m_conv, m_rec = jnp.split(proj, split_idx, axis=-1)
        y_conv = conformer_branch(glu_a, glu_b, z_conv, conv_w[l], conv_b[l],
                                  ln_g[l], ln_b[l], w_conv_out[l])
        y_rec = hgrn2_branch(q, f, i, g_rec, lb_all[l], gnorm_g[l], w_rec_out[l])
        merged = jax.nn.sigmoid(m_conv) * y_conv + jax.nn.sigmoid(m_rec) * y_rec
        h_res = h_res + merged @ w_out[l]
    return rmsnorm(h_res[:, N_META:], final_g)
```

```python
import numpy as np
import concourse.bass as bass
import concourse.mybir as mybir
from concourse.bass_utils import run_bass_kernel_spmd

F32 = mybir.dt.float32
BF16 = mybir.dt.bfloat16
AF = mybir.ActivationFunctionType
ALU = mybir.AluOpType

ENGS = ("pe", "act", "dve", "pool", "sp")


class Buf:
    _n = 0

    def __init__(self, t, name=None):
        self.t = t
        Buf._n += 1
        self.key = (name or "buf", Buf._n)

    def __getitem__(self, k):
        return self.t[k]


class Sched:
    def __init__(self, nc, n_dma_sems=8):
        self.nc = nc
        self.eng = dict(pe=nc.tensor, act=nc.scalar, dve=nc.vector, pool=nc.gpsimd, sp=nc.sync)
        self.ops = {e: [] for e in ENGS}
        self.last_w = {}
        self.readers = {}
        self.n_dma_sems = n_dma_sems
        self.dma_count = {e: 0 for e in ENGS}
        self.dma_sem_uses = {}
        self.dma_last = {}

    @staticmethod
    def _k(b):
        if isinstance(b, Buf):
            return b.key
        if isinstance(b, tuple) and len(b) > 0 and isinstance(b[0], Buf):
            return (b[0].key,) + tuple(b[1:])
        return b

    def op(self, e, fn, reads=(), writes=(), dma=False):
        idx = len(self.ops[e])
        deps = []
        rk = [self._k(b) for b in reads]
        wk = [self._k(b) for b in writes]
        for k in rk:
            lw = self.last_w.get(k)
            if lw is not None:
                deps.append(lw)
        for k in wk:
            lw = self.last_w.get(k)
            if lw is not None:
                deps.append(lw)
            deps.extend(self.readers.get(k, ()))
        rec = dict(eng=e, idx=idx, fn=fn, deps=deps, dma=dma, signal=False)
        if dma:
            slot = self.dma_count[e] % self.n_dma_sems
            self.dma_count[e] += 1
            n = self.dma_sem_uses.get((e, slot), 0) + 1
            self.dma_sem_uses[(e, slot)] = n
            rec["dslot"] = (e, slot)
            rec["dval"] = 16 * n
            prev = self.dma_last.get((e, slot))
            if prev is not None:
                deps.append(prev)
            tok = ("dma", e, slot, 16 * n)
            self.dma_last[(e, slot)] = tok
        else:
            tok = ("eng", e, idx)
        rec["tok"] = tok
        self.ops[e].append(rec)
        for k in rk:
            self.readers.setdefault(k, []).append(tok)
        for k in wk:
            self.last_w[k] = tok
            self.readers[k] = []
        return tok

    def emit(self, final_waits=()):
        nc = self.nc
        need = {e: [] for e in ENGS}
        for e in ENGS:
            seen = {}
            for rec in self.ops[e]:
                w = {}
                for d in rec["deps"]:
                    if d[0] == "eng":
                        _, pe_, pidx = d
                        if pe_ == e:
                            if e in ("pe", "sp"):
                                continue
                            if pidx >= rec["idx"]:
                                continue
                        key = ("eng", pe_)
                        val = pidx
                    else:
                        _, qe, slot, v = d
                        key = ("dma", qe, slot)
                        val = v
                    if seen.get(key, -1) >= val:
                        continue
                    if w.get(key, -1) < val:
                        w[key] = val
                for key, val in w.items():
                    seen[key] = val
                    if key[0] == "eng":
                        self.ops[key[1]][val]["signal"] = True
                rec["waits"] = w
        for e in ENGS:
            c = 0
            for rec in self.ops[e]:
                if rec["signal"]:
                    c += 1
                    rec["sigval"] = c
        esem = {e: nc.alloc_semaphore("s_" + e) for e in ENGS}
        dsem = {}
        for (e, slot) in self.dma_sem_uses:
            dsem[(e, slot)] = nc.alloc_semaphore("d_%s%d" % (e, slot))
        for e in ENGS:
            eh = self.eng[e]
            for rec in self.ops[e]:
                for key, val in rec["waits"].items():
                    if key[0] == "eng":
                        sv = self.ops[key[1]][val]["sigval"]
                        eh.wait_ge(esem[key[1]], sv)
                    else:
                        eh.wait_ge(dsem[(key[1], key[2])], val)
                ins = rec["fn"](eh)
                if rec["dma"]:
                    ins.then_inc(dsem[rec["dslot"]], 16)
                elif rec["signal"]:
                    ins.then_inc(esem[e], 1)
        eh = self.eng["sp"]
        for tok in final_waits:
            _, qe, slot, v = tok
            eh.wait_ge(dsem[(qe, slot)], v)
        n = {e: len(self.ops[e]) for e in ENGS}
        return n


D = 1024
KC = 8
H = 8
NMETA = 16
CW = 31
EPS = 1e-6
NUNIT = 24

_SEG = dict(glu_a=0, glu_b=1024, z=2048, q=3072, f=4096, i=5120, g=6144, mc=7168, mr=8192,
            wco=9216, wro=10240, wo=11264)
_UNIT_COLS = [
    ("glu_b", 0), ("glu_a", 0), ("glu_b", 1), ("glu_a", 1),
    ("z", 0), ("z", 1),
    ("mc", 0), ("wco", 0), ("mc", 1), ("wco", 1),
    ("q", 0), ("q", 1), ("f", 0), ("f", 1),
    ("i", 0), ("i", 1), ("g", 0), ("g", 1),
    ("mr", 0), ("wro", 0), ("mr", 1), ("wro", 1),
    ("wo", 0), ("wo", 1),
]


def host_layout(inputs):
    f32 = np.float32
    w_in = np.asarray(inputs["w_in"], f32)[0]
    wcat = np.concatenate([w_in, np.asarray(inputs["w_conv_out"], f32)[0],
                           np.asarray(inputs["w_rec_out"], f32)[0], np.asarray(inputs["w_out"], f32)[0]], axis=1)
    wall = np.empty((NUNIT, 128, KC, 512), f32)
    for u, (seg, half) in enumerate(_UNIT_COLS):
        c0 = _SEG[seg] + 512 * half
        blk = wcat[:, c0:c0 + 512]
        wall[u] = blk.reshape(KC, 128, 512).transpose(1, 0, 2)
    wall = wall.reshape(NUNIT, 128, KC * 512)
    fm = lambda v: np.ascontiguousarray(np.asarray(v, f32).reshape(KC, 128).T)
    lbl = np.asarray(inputs["lb_logits"], f32)
    vecs = np.concatenate([fm(inputs["norm_g"][0]), fm(inputs["conv_b"][0]), fm(inputs["ln_g"][0]),
                           fm(inputs["ln_b"][0]), fm(inputs["gnorm_g"][0]), fm(lbl[0]), fm(lbl[1])], axis=1)
    cw = np.asarray(inputs["conv_w"], f32)[0]
    convw = np.ascontiguousarray(cw.reshape(CW, KC, 128).transpose(2, 1, 0)).reshape(128, KC * CW)
    gbc = np.ascontiguousarray(np.broadcast_to(np.asarray(inputs["norm_g"], f32)[0][None, :], (128, D)))
    fgbc = np.ascontiguousarray(np.broadcast_to(np.asarray(inputs["final_g"], f32)[None, :], (128, D)))
    return dict(wall=wall, vecs=np.ascontiguousarray(vecs), convw=convw, gbc=gbc, fgbc=fgbc,
                meta=np.ascontiguousarray(np.asarray(inputs["meta_tokens"], f32)))


class Ring:
    def __init__(self, mk, name, shape, dt, n):
        self.bufs = [mk("%s%d" % (name, i), shape, dt) for i in range(n)]
        self.i = 0

    def next(self):
        b = self.bufs[self.i % len(self.bufs)]
        self.i += 1
        return b


def build_program(SEQ, NB, n_dma_sems=8):
    nc = bass.Bass("TRN2", target_bir_lowering=False)
    S = Sched(nc, n_dma_sems=n_dma_sems)
    NT = NB // 128
    NCH = NB // 64
    NBLK = SEQ // NB
    assert NB % 128 == 0 and SEQ % NB == 0

    def dram(name, shape, dt, kind):
        return nc.dram_tensor(name, list(shape), dt, kind=kind).ap()
    x_d = dram("x", [SEQ, D], F32, "ExternalInput")
    meta_d = dram("meta", [NMETA, D], F32, "ExternalInput")
    wall_d = dram("wall", [NUNIT, 128, KC * 512], F32, "ExternalInput")
    vecs_d = dram("vecs", [128, 56], F32, "ExternalInput")
    convw_d = dram("convw", [128, KC * CW], F32, "ExternalInput")
    gbc_d = dram("gbc", [128, D], F32, "ExternalInput")
    fgbc_d = dram("fgbc", [128, D], F32, "ExternalInput")
    out_d = dram("out", [SEQ, D], F32, "ExternalOutput")
    wscr_d = nc.dram_tensor("wscr", [NUNIT, 128, KC * 512], BF16, kind="Internal").ap()
    dscr_d = nc.dram_tensor("dscr", [KC, 128, CW * 128], BF16, kind="Internal").ap()

    def sb(name, shape, dt):
        return Buf(nc.alloc_sbuf_tensor("s_" + name, list(shape), dt), name)

    def psb(name, shape, dt=F32):
        return Buf(nc.alloc_psum_tensor("p_" + name, list(shape), dt), name)

    def dma(eng, out, in_, reads=(), writes=()):
        return S.op(eng, lambda e: e.dma_start(out=out, in_=in_), reads, writes, dma=True)

    def act(out, in_, func, reads, writes, scale=1.0, bias=None, accum=None):
        kw = dict(out=out, in_=in_, func=func, scale=scale)
        if bias is not None:
            kw["bias"] = bias
        if accum is not None:
            kw["accum_out"] = accum
        return S.op("act", lambda e: e.activation(**kw), reads, writes)

    def tt(eng, out, in0, in1, op, reads, writes):
        return S.op(eng, lambda e: e.tensor_tensor(out=out, in0=in0, in1=in1, op=op), reads, writes)

    def ts(eng, out, in0, s1, s2, op0, op1, reads, writes):
        if s2 is None:
            return S.op(eng, lambda e: e.tensor_scalar(out=out, in0=in0, scalar1=s1, scalar2=None, op0=op0), reads, writes)
        return S.op(eng, lambda e: e.tensor_scalar(out=out, in0=in0, scalar1=s1, scalar2=s2, op0=op0, op1=op1), reads, writes)

    def stt(eng, out, in0, scalar, in1, op0, op1, reads, writes):
        return S.op(eng, lambda e: e.scalar_tensor_tensor(out=out, in0=in0, scalar=scalar, in1=in1, op0=op0, op1=op1),
                    reads, writes)

    def cp(eng, out, in_, reads, writes):
        if eng == "act":
            return S.op("act", lambda e: e.copy(out=out, in_=in_), reads, writes)
        return S.op(eng, lambda e: e.tensor_copy(out=out, in_=in_), reads, writes)

    def mm(out, lhsT, rhs, start, stop, reads, writes, skip=False):
        if skip:
            return S.op("pe", lambda e: e.matmul(out, lhsT=lhsT, rhs=rhs, start=start, stop=stop,
                                                 skip_group_check=True), reads, writes)
        return S.op("pe", lambda e: e.matmul(out, lhsT=lhsT, rhs=rhs, start=start, stop=stop), reads, writes)

    def tr(out, in_, ident_ap, reads, writes):
        return S.op("pe", lambda e: e.transpose(out=out, in_=in_, identity=ident_ap), reads, writes)

    def memset(eng, ap, val, reads, writes):
        return S.op(eng, lambda e: e.memset(ap, val), reads, writes)

    ident32 = sb("ident32", [128, 128], F32)
    ident = sb("ident", [128, 128], BF16)
    ones = sb("ones", [128, 128], BF16)
    pmask = sb("pmask", [128, 128], F32)
    smask = sb("smask", [128, NB], F32)
    epst = sb("epst", [128, 1], F32)
    vecs = sb("vecs", [128, 56], F32)
    lbt = sb("lbt", [128, 32], F32)
    convw = sb("convw", [128, KC, CW], F32)
    convw16 = sb("convw16", [128, KC, CW], BF16)
    gbc = sb("gbc", [128, D], F32)
    fgbc = sb("fgbc", [128, D], F32)
    wring = Ring(sb, "wb", [128, KC, 512], BF16, 5)
    xin = Ring(sb, "xin", [128, D], F32, 2)
    junk = sb("junk", [128, D], BF16)
    small = Ring(sb, "small", [128, 4], F32, 4)
    xhr = Ring(sb, "xh", [128, D], BF16, 2)
    hTs = [sb("hT%d" % i, [128, KC, NB], BF16) for i in range(2)]
    hTm = sb("hTm", [128, KC, NMETA], BF16)
    aT = sb("aT", [128, KC, 30 + NB], BF16)
    halo = sb("halo", [128, KC, 30], BF16)
    sgr = Ring(sb, "sg", [128, NB], F32, 4)
    dgr = Ring(sb, "dg", [128, CW, 128], BF16, 2)
    cfull = sb("cfull", [128, KC, NB], F32)
    c16r = Ring(sb, "c16", [128, NB], BF16, 2)
    q16r = Ring(sb, "q16", [128, NB], BF16, 2)
    stat = [sb("stat%d" % i, [128, NB], F32) for i in range(4)]
    gated = sb("gated", [128, KC, NB], BF16)
    ycg = sb("ycg", [128, KC, NB], BF16)
    qf = sb("qf", [128, H, NB], F32)
    t1r = Ring(sb, "t1", [128, NB], F32, 4)
    t2r = Ring(sb, "t2", [128, NB], F32, 4)
    t3r = Ring(sb, "t3", [128, NB], F32, 4)
    t4r = Ring(sb, "t4", [128, NB], F32, 4)
    khall = sb("khall", [128, H, NB], BF16)
    ebl = sb("ebl", [128, H, NCH, 1], F32)
    qtT = sb("qtT", [128, H, NB], BF16)
    ktT = sb("ktT", [128, H, NB], BF16)
    khat = sb("khat", [128, NT, H, 128], BF16)
    vtm = sb("vtm", [128, NT, D], BF16)
    khm = sb("khm", [128, H, NMETA], BF16)
    khatm = sb("khatm", [NMETA, H, 128], BF16)
    vm = sb("vm", [NMETA, D], BF16)
    atr = Ring(sb, "at", [128, H, 128], BF16, 2)
    Sf = sb("Sf", [128, H, 128], F32)
    Sb = [sb("Sb%d" % i, [128, H, 128], BF16) for i in range(2)]
    of32 = sb("of32", [128, H, NB], F32)
    graw = sb("graw", [128, H, NB], F32)
    ror = Ring(sb, "ro", [128, NB], F32, 2)
    og = sb("og", [128, KC, NB], BF16)
    merged = sb("merged", [128, KC, NB], BF16)
    xrr = xin
    pm = Ring(psb, "pm", [128, 512], F32, 4)
    pd = [psb("pd%d" % i, [128, 512], F32) for i in range(2)]
    pst = psb("pst", [128, 512], F32)
    ptr = Ring(psb, "pt", [128, KC, 128], BF16, 1)

    ng, cb, lng, lnb, gng = (vecs[:, 8 * i:8 * i + 8] for i in range(5))
    lb, oml, noml = lbt[:, 8:16], lbt[:, 16:24], lbt[:, 24:32]

    dma("sp", vecs[:], vecs_d, writes=[vecs])
    dma("sp", convw[:], convw_d.rearrange("p (c j) -> p c j", c=KC), writes=[convw])
    dma("sp", gbc[:], gbc_d, writes=[gbc])
    dma("sp", fgbc[:], fgbc_d, writes=[fgbc])
    memset("pool", ident32[:], 0.0, [], [ident32])
    S.op("pool", lambda e: e.affine_select(out=ident32[:], in_=ident32[:], pattern=[[-1, 128]], compare_op=ALU.not_equal,
                                            fill=1.0, base=0, channel_multiplier=1), [ident32], [ident32])
    cp("pool", ident[:], ident32[:], [ident32], [ident])
    memset("pool", ones[:], 1.0, [], [ones])
    memset("pool", pmask[:], 1.0, [], [pmask])
    S.op("pool", lambda e: e.affine_select(out=pmask[:], in_=pmask[:], pattern=[[1, 128]], compare_op=ALU.is_ge,
                                            fill=0.0, base=0, channel_multiplier=-1), [pmask], [pmask])
    memset("pool", pmask[0:64, 64:128], 0.0, [pmask], [pmask])
    memset("pool", smask[:], 1.0, [], [smask])
    memset("pool", smask[:].rearrange("p (c j) -> p c j", j=64)[:, :, 0:1], 0.0, [smask], [smask])
    memset("pool", epst[:], EPS, [], [epst])
    memset("pool", halo[:], 0.0, [], [halo])
    tt("dve", lbt[:, 0:8], vecs[:, 40:48], vecs[:, 48:56], ALU.subtract, [vecs], [(lbt, 0)])
    act(lbt[:, 8:16], lbt[:, 0:8], AF.Sigmoid, [(lbt, 0)], [(lbt, 1)])
    act(lbt[:, 16:24], lbt[:, 0:8], AF.Sigmoid, [(lbt, 0)], [(lbt, 2)], scale=-1.0)
    ts("dve", lbt[:, 24:32], lbt[:, 16:24], -1.0, None, ALU.mult, None, [(lbt, 2)], [(lbt, 3)])
    LBK = [(lbt, 1), (lbt, 2), (lbt, 3)]
    cp("dve", convw16[:], convw[:], [convw], [convw16])

    for c_ in range(KC):
        dg_ = dgr.next()
        tt("dve", dg_[:], ident[:].unsqueeze(1).to_broadcast([128, CW, 128]),
           convw16[:, c_, :].unsqueeze(2).to_broadcast([128, CW, 128]), ALU.mult, [ident, convw16], [dg_])
        dma("sp", dscr_d[c_].rearrange("p (j m) -> p j m", j=CW), dg_[:], reads=[dg_], writes=[("dscr", c_)])

    seq = [0, 1, 2, 3, 12, 13, 14, 15] + [0, 1, 2, 3]
    for n_ in range(NBLK):
        seq += list(range(4, 16)) + ([0, 1, 2, 3] if n_ + 1 < NBLK else []) + list(range(16, NUNIT))
    converted = set()
    cast_i = [0]
    state = dict(issued=0, taken=0, bufs={})

    def convert(u):
        if u in converted:
            return
        converted.add(u)
        dma("pool", wscr_d[u], wall_d[u], writes=[("wscr", u)])
    for u_ in seq[:8 + NUNIT]:
        convert(u_)

    def issue_upto(i):
        while state["issued"] <= i and state["issued"] < len(seq):
            u = seq[state["issued"]]
            convert(u)
            b = wring.next()
            dma("sp", b[:], wscr_d[u].rearrange("p (k n) -> p k n", k=KC),
                reads=[("wscr", u)], writes=[b])
            state["bufs"][state["issued"]] = b
            state["issued"] += 1

    def wget():
        i = state["taken"]
        issue_upto(i + 3)
        state["taken"] += 1
        return state["bufs"].pop(i)

    def load_norm_transpose(src, ntok, dst, col0, key):
        norm_transpose_b(norm_transpose_a(src, ntok), ntok, dst, col0, key)

    def norm_transpose_a(src, ntok):
        xt = xin.next()
        dma("sp", xt[0:ntok, :], src, writes=[xt])
        ss = small.next()
        act(junk[0:ntok, :], xt[0:ntok, :], AF.Square, [xt], [junk, (ss, 0)], accum=ss[0:ntok, 0:1])
        act(ss[0:ntok, 1:2], ss[0:ntok, 0:1], AF.Ln, [(ss, 0), epst], [(ss, 1)], scale=1.0 / D, bias=epst[0:ntok, 0:1])
        act(ss[0:ntok, 2:3], ss[0:ntok, 1:2], AF.Exp, [(ss, 1)], [(ss, 2)], scale=-0.5)
        xh = xhr.next()
        stt("dve", xh[0:ntok, :], xt[0:ntok, :], ss[0:ntok, 2:3], gbc[0:ntok, :], ALU.mult, ALU.mult,
            [xt, (ss, 2), gbc], [xh])
        return xh

    def norm_transpose_b(xh, ntok, dst, col0, key):
        pt = ptr.next()
        for k in range(KC):
            tr(pt[:, k, 0:ntok], xh[0:ntok, k * 128:(k + 1) * 128], ident[0:ntok, 0:ntok], [xh, ident], [pt])
        cp("dve", dst[:, :, col0:col0 + ntok], pt[:, :, 0:ntok], [pt], [key])

    def mm_fm(ps_ap, psbuf, w, j, actT, ntok, akeys):
        for k in range(KC):
            mm(ps_ap, w[:, k, j * 128:(j + 1) * 128], actT[:, k, 0:ntok], k == 0, k == KC - 1, [w] + akeys, [psbuf])

    def glu_stage(actT, ntok, akeys, dst_fn, dst_key_fn):
        for _ in glu_steps(actT, ntok, akeys, dst_fn, dst_key_fn):
            pass

    def glu_steps(actT, ntok, akeys, dst_fn, dst_key_fn):
        for uc in range(2):
            wb = wget()
            wa = wget()
            for j in range(4):
                c = 4 * uc + j
                pb = pm.next()
                mm_fm(pb[:, 0:ntok], pb, wb, j, actT, ntok, akeys)
                sg = sgr.next()
                act(sg[:, 0:ntok], pb[:, 0:ntok], AF.Sigmoid, [pb], [sg])
                pa = pm.next()
                mm_fm(pa[:, 0:ntok], pa, wa, j, actT, ntok, akeys)
                tt("dve", dst_fn(c), pa[:, 0:ntok], sg[:, 0:ntok], ALU.mult, [pa, sg], [dst_key_fn(c)])
                yield

    def f_head(pf, ntok, h, full, sig=None, sigk=None):
        t1, t2, t3, t4 = t1r.next(), t2r.next(), t3r.next(), t4r.next()
        if sig is None:
            act(t1[:, 0:ntok], pf[:, 0:ntok], AF.Sigmoid, [pf], [t1])
            sig, sigk = t1[:, 0:ntok], [t1]
        ts("dve", t2[:, 0:ntok], sig, noml[:, h:h + 1], oml[:, h:h + 1], ALU.mult, ALU.add, sigk + LBK, [t2])
        act(t1[:, 0:ntok], sig, AF.Ln, sigk + LBK, [t1], scale=oml[:, h:h + 1], bias=lb[:, h:h + 1])
        S.op("dve", lambda e: e.tensor_tensor_scan(out=t3[:, 0:ntok], data0=smask[:, 0:ntok], data1=t1[:, 0:ntok],
                                                   initial=0.0, op0=ALU.mult, op1=ALU.add), [smask, t1], [t3])
        cl = min(64, ntok)
        t3v = t3[:, 0:ntok].rearrange("p (c j) -> p c j", j=cl)
        t4v = t4[:, 0:ntok].rearrange("p (c j) -> p c j", j=cl)
        nch = ntok // cl
        if full:
            act(t1[:, 0:ntok], t3[:, 0:ntok], AF.Exp, [t3], [t1])
            t1v = t1[:, 0:ntok].rearrange("p (c j) -> p c j", j=cl)
            cp("pool", ebl[:, h, :, :], t1v[:, :, cl - 1:cl], [t1], [(ebl, h)])
            tt("dve", qtT[:, h, :], qf[:, h, :], t1[:, 0:ntok], ALU.mult, [(qf, h), t1], [(qtT, h)])
            act(t4[:, 0:ntok], t3[:, 0:ntok], AF.Exp, [t3], [t4], scale=-1.0)
            tt("dve", ktT[:, h, :], t2[:, 0:ntok], t4[:, 0:ntok], ALU.mult, [t2, t4], [(ktT, h)])
        tt("dve", t4v, t3v, t3v[:, :, cl - 1:cl].to_broadcast([128, nch, cl]), ALU.subtract, [t3, t4], [t4])
        act(t4[:, 0:ntok], t4[:, 0:ntok], AF.Exp, [t4], [t4], scale=-1.0)
        return t2, t4

    load_norm_transpose(meta_d, NMETA, hTm, 0, hTm)
    glu_stage(hTm, NMETA, [hTm], lambda c: halo[:, c, 30 - NMETA:30], lambda c: halo)
    ptm = ptr.next()
    for uf in range(2):
        wf = wget()
        for j in range(4):
            h = 4 * uf + j
            pf = pm.next()
            mm_fm(pf[:, 0:NMETA], pf, wf, j, hTm, NMETA, [hTm])
            act(of32[:, h, 0:NMETA], pf[:, 0:NMETA], AF.Sigmoid, [pf], [("msig", h)])
    for h in range(H):
        t2, t4 = f_head(None, NMETA, h, False, sig=of32[:, h, 0:NMETA], sigk=[("msig", h)])
        tt("dve", khm[:, h, :], t2[:, 0:NMETA], t4[:, 0:NMETA], ALU.mult, [t2, t4], [(khm, h)])
        tr(ptm[0:NMETA, h, :], khm[:, h, :], ident[:], [(khm, h), ident], [ptm])
    cp("dve", khatm[:], ptm[0:NMETA, :, :], [ptm], [khatm])
    wi = [wget(), wget()]
    for half in range(2):
        pv = pm.next()
        for k in range(KC):
            mm(pv[0:NMETA, :], hTm[:, k, :], wi[half][:, k, :], k == 0, k == KC - 1, [hTm, wi[half]], [pv])
        cp("act", vm[:, half * 512:(half + 1) * 512], pv[0:NMETA, :], [pv], [vm])
    pu = [pm.next(), pm.next()]
    for h in range(H):
        mm(pu[h // 4][:, (h % 4) * 128:(h % 4 + 1) * 128], khatm[:, h, :], vm[:, h * 128:(h + 1) * 128],
           True, True, [khatm, vm], [pu[h // 4]])
    cur = 0
    for g in range(2):
        cp("dve", Sf[:, 4 * g:4 * g + 4, :], pu[g][:].rearrange("p (g t) -> p g t", g=4), [pu[g]], [(Sf, g)])
        cp("act", Sb[cur][:, 4 * g:4 * g + 4, :], Sf[:, 4 * g:4 * g + 4, :], [(Sf, g)], [(Sb[cur], g)])

    finals = []
    aTk = [(aT, c) for c in range(KC)]
    hT = hTs[0]
    for t in range(NT):
        load_norm_transpose(x_d[t * 128:(t + 1) * 128, :], 128, hT, t * 128, (hT, t))
    assert 2 * NB <= 512
    sk = lambda h: [("sigf", h), ("msig", h)] + [(of32, h // 4, p) for p in range(NT)]
    v3 = lambda ap: ap.rearrange("p (c j) -> p c j", j=64)
    mu, msq, var, rstd = (s_[:] for s_ in stat)
    cur_box = [cur]

    def C1(n):
        hT = hTs[n % 2]
        hTk = [(hT, t) for t in range(NT)]
        cp("pool", aT[:, :, 0:30], halo[:], [halo], [(aT, "h")])
        glu_stage(hT, NB, hTk, lambda c: aT[:, c, 30:30 + NB], lambda c: (aT, c))
        yield
        dgs = {}

        def build_dg(c):
            dg = dgr.next()
            dma("sp", dg[:], dscr_d[c].rearrange("p (j m) -> p j m", j=CW), reads=[("dscr", c)], writes=[dg])
            dgs[c] = dg
        build_dg(0)
        memset("dve", pst[:], 0.0, [], [pst])
        prev = None
        for c in range(KC):
            if c + 1 < KC:
                build_dg(c + 1)
            dg = dgs.pop(c)
            pc = pm.next()
            for j in range(CW):
                mm(pc[:, 0:NB], dg[:, j, :], aT[:, c, j:j + NB], j == 0, j == CW - 1, [dg, (aT, c), (aT, "h")], [pc])
            act(cfull[:, c, :], pc[:, 0:NB], AF.Identity, [pc, vecs], [(cfull, c)], bias=cb[:, c:c + 1])
            c16 = c16r.next()
            q16 = q16r.next()
            act(q16[:], pc[:, 0:NB], AF.Square, [pc, vecs], [q16], bias=cb[:, c:c + 1])
            act(c16[:], pc[:, 0:NB], AF.Identity, [pc, vecs], [c16], bias=cb[:, c:c + 1])
            if prev is not None:
                pc_, c16_, q16_ = prev
                mm(pst[:, 0:NB], ones[:], c16_[:], False, False, [ones, c16_], [pst], skip=True)
                mm(pst[:, NB:2 * NB], ones[:], q16_[:], False, False, [ones, q16_], [pst], skip=True)
            prev = (c, c16, q16)
            yield
        pc_, c16_, q16_ = prev
        mm(pst[:, 0:NB], ones[:], c16_[:], False, True, [ones, c16_], [pst], skip=True)
        mm(pst[:, NB:2 * NB], ones[:], q16_[:], False, True, [ones, q16_], [pst], skip=True)
        cp("pool", halo[:], aT[:, :, NB:NB + 30], aTk, [halo])
        S.op("act", lambda e: e.mul(out=mu, in_=pst[:, 0:NB], mul=1.0 / D), [pst], [stat[0]])
        tt("dve", msq, mu, mu, ALU.mult, [stat[0]], [stat[1]])
        stt("dve", var, pst[:, NB:2 * NB], 1.0 / D, msq, ALU.mult, ALU.subtract, [pst, stat[1]], [stat[2]])
        act(var, var, AF.Ln, [stat[2], epst], [stat[2]], bias=epst[:, 0:1])
        act(rstd, var, AF.Exp, [stat[2]], [stat[3]], scale=-0.5)
        yield

    def ln_norm(c):
        tt("dve", cfull[:, c, :], cfull[:, c, :], mu, ALU.subtract, [(cfull, c), stat[0]], [(cfull, c)])
        tt("dve", cfull[:, c, :], cfull[:, c, :], rstd, ALU.mult, [(cfull, c), stat[3]], [(cfull, c)])
        act(cfull[:, c, :], cfull[:, c, :], AF.Silu, [(cfull, c), vecs], [(cfull, c)],
            scale=lng[:, c:c + 1], bias=lnb[:, c:c + 1])

    def C2R1(n):
        t0 = n * NB
        hT = hTs[n % 2]
        hTn = hTs[(n + 1) % 2]
        hTk = [(hT, t) for t in range(NT)]
        for uz in range(2):
            wz = wget()
            for j in range(4):
                c = 4 * uz + j
                pz = pm.next()
                mm_fm(pz[:, 0:NB], pz, wz, j, hT, NB, hTk)
                sz = sgr.next()
                act(sz[:], pz[:, 0:NB], AF.Silu, [pz], [sz])
                tt("dve", gated[:, c, :], cfull[:, c, :], sz[:], ALU.mult, [(cfull, c), sz], [(gated, c)])
        gk = [(gated, c) for c in range(KC)]
        for uo in range(2):
            wm = wget()
            wo = wget()
            sms = []
            for j in range(4):
                pmc = pm.next()
                mm_fm(pmc[:, 0:NB], pmc, wm, j, hT, NB, hTk)
                sm = sgr.next()
                act(sm[:], pmc[:, 0:NB], AF.Sigmoid, [pmc], [sm])
                sms.append(sm)
            for j in range(4):
                m = 4 * uo + j
                sm = sms[j]
                py = pm.next()
                for k in range(KC):
                    mm(py[:, 0:NB], wo[:, k, j * 128:(j + 1) * 128], gated[:, k, :], k == 0, k == KC - 1, [wo] + gk, [py])
                tt("dve", ycg[:, m, :], py[:, 0:NB], sm[:], ALU.mult, [py, sm], [(ycg, m)])
        xhn = []
        if n + 1 < NBLK:
            for t in range(NT):
                xhn.append(norm_transpose_a(x_d[t0 + NB + t * 128:t0 + NB + (t + 1) * 128, :], 128))
        for uq in range(2):
            wq = wget()
            for j in range(4):
                h = 4 * uq + j
                pq = pm.next()
                mm_fm(pq[:, 0:NB], pq, wq, j, hT, NB, hTk)
                act(qf[:, h, :], pq[:, 0:NB], AF.Silu, [pq], [(qf, h)])
        for uf in range(2):
            wf = wget()
            for j in range(4):
                h = 4 * uf + j
                pf = pm.next()
                mm_fm(pf[:, 0:NB], pf, wf, j, hT, NB, hTk)
                act(of32[:, h, :], pf[:, 0:NB], AF.Sigmoid, [pf], sk(h))
        wi = [wget(), wget()]
        for t in range(NT):
            for half in range(2):
                pv = pm.next()
                for k in range(KC):
                    mm(pv[:], hT[:, k, t * 128:(t + 1) * 128], wi[half][:, k, :], k == 0, k == KC - 1,
                       [(hT, t), wi[half]], [pv])
                cp("act", vtm[:, t, half * 512:(half + 1) * 512], pv[:], [pv], [(vtm, t, half)])
        for t, xh_ in enumerate(xhn):
            norm_transpose_b(xh_, 128, hTn, t * 128, (hTn, t))

    def R2(n):
        for g0 in (0, 4):
            G = range(g0, g0 + 4)
            T = {h: (t1r.next(), t2r.next(), t3r.next(), t4r.next()) for h in G}
            for h in G:
                t1, t2, t3, t4 = T[h]
                ts("dve", t2[:], of32[:, h, :], noml[:, h:h + 1], oml[:, h:h + 1], ALU.mult, ALU.add, sk(h) + LBK, [t2])
                act(t1[:], of32[:, h, :], AF.Ln, sk(h) + LBK, [t1], scale=oml[:, h:h + 1], bias=lb[:, h:h + 1])
            yield
            for h in G:
                t1, t2, t3, t4 = T[h]
                S.op("dve", (lambda t3=t3, t1=t1: lambda e: e.tensor_tensor_scan(
                    out=t3[:], data0=smask[:], data1=t1[:], initial=0.0, op0=ALU.mult, op1=ALU.add))(), [smask, t1], [t3])
            yield
            for h in G:
                t1, t2, t3, t4 = T[h]
                act(t1[:], t3[:], AF.Exp, [t3], [t1])
                cp("pool", ebl[:, h, :, :], v3(t1[:])[:, :, 63:64], [t1], [(ebl, h)])
                tt("dve", qtT[:, h, :], qf[:, h, :], t1[:], ALU.mult, [(qf, h), t1], [(qtT, h)])
            yield
            for h in G:
                t1, t2, t3, t4 = T[h]
                act(t4[:], t3[:], AF.Exp, [t3], [t4], scale=-1.0)
                tt("dve", ktT[:, h, :], t2[:], t4[:], ALU.mult, [t2, t4], [(ktT, h)])
            yield
            for h in G:
                t1, t2, t3, t4 = T[h]
                tt("dve", v3(t4[:]), v3(t3[:]), v3(t3[:])[:, :, 63:64].to_broadcast([128, NCH, 64]), ALU.subtract,
                   [t3, t4], [t4])
                act(t4[:], t4[:], AF.Exp, [t4], [t4], scale=-1.0)
                tt("dve", khall[:, h, :], t2[:], t4[:], ALU.mult, [t2, t4], [(khall, h)])
            yield
        assert NT * 4 <= KC
        for g in range(2):
            pt = ptr.next()
            for hh in range(4):
                h = 4 * g + hh
                for t in range(NT):
                    tr(pt[:, t * 4 + hh, :], khall[:, h, t * 128:(t + 1) * 128], ident[:], [(khall, h), ident], [pt])
            cp("act", khat[:, :, 4 * g:4 * g + 4, :], pt[:, 0:NT * 4, :].rearrange("p (t h) m -> p t h m", t=NT),
               [pt], [(khat, 4 * g + hh) for hh in range(4)])
            yield
        for p in range(NT):
            vk = [(vtm, p, 0), (vtm, p, 1)]
            pa = [pm.next(), pm.next()]
            for h in range(H):
                mm(pa[h // 4][:, (h % 4) * 128:(h % 4 + 1) * 128], ktT[:, h, p * 128:(p + 1) * 128],
                   qtT[:, h, p * 128:(p + 1) * 128], True, True, [(ktT, h), (qtT, h)], [pa[h // 4]])
            at = atr.next()
            for g in range(2):
                tt("dve", at[:, 4 * g:4 * g + 4, :], pa[g][:].rearrange("p (g t) -> p g t", g=4),
                   pmask[:].unsqueeze(1).to_broadcast([128, 4, 128]), ALU.mult, [pa[g], pmask], [(at, g)])
            yield
            pus = {}

            def u_mm(ci):
                r0 = 64 * ci
                pu = [pm.next(), pm.next()]
                for h in range(H):
                    mm(pu[h // 4][:, (h % 4) * 128:(h % 4 + 1) * 128], khat[r0:r0 + 64, p, h, :],
                       vtm[r0:r0 + 64, p, h * 128:(h + 1) * 128], True, True, vk + [(khat, h)], [pu[h // 4]])
                pus[ci] = pu

            def o_mm(ci):
                cur = cur_box[0]
                r0 = 64 * ci
                for h in range(H):
                    po = pd[h // 4]
                    c0 = (h % 4) * 128 + r0
                    mm(po[:, c0:c0 + 64], vtm[r0:r0 + 64, p, h * 128:(h + 1) * 128], at[r0:r0 + 64, h, r0:r0 + 64],
                       True, False, vk + [(at, h // 4)], [po])
                    mm(po[:, c0:c0 + 64], Sb[cur][:, h, :], qtT[:, h, p * 128 + r0:p * 128 + r0 + 64],
                       False, True, [(Sb[cur], h // 4), (qtT, h)], [po])

            def s_update(ci):
                cur = cur_box[0]
                gch = 2 * p + ci
                pu = pus[ci]
                nxt = 1 - cur
                for g in range(2):
                    hs = slice(4 * g, 4 * g + 4)
                    tt("dve", Sf[:, hs, :], Sf[:, hs, :], ebl[:, hs, gch, :].to_broadcast([128, 4, 128]), ALU.mult,
                       [(Sf, g)] + [(ebl, h) for h in range(4 * g, 4 * g + 4)], [(Sf, g)])
                    tt("dve", Sf[:, hs, :], Sf[:, hs, :], pu[g][:].rearrange("p (g t) -> p g t", g=4), ALU.add,
                       [(Sf, g), pu[g]], [(Sf, g)])
                    cp("act", Sb[nxt][:, hs, :], Sf[:, hs, :], [(Sf, g)], [(Sb[nxt], g)])
                cur_box[0] = nxt

            o_mm(0)
            yield
            u_mm(0)
            s_update(0)
            yield
            o_mm(1)
            yield
            u_mm(1)
            s_update(1)
            for g in range(2):
                cp("act", of32[:, 4 * g:4 * g + 4, p * 128:(p + 1) * 128], pd[g][:].rearrange("p (g t) -> p g t", g=4),
                   [pd[g]], [(of32, g, p)] + [("sigf", 4 * g + q) for q in range(4)])
            yield

    def R3(n):
        t0 = n * NB
        hT = hTs[n % 2]
        hTk = [(hT, t) for t in range(NT)]
        xrs = []
        for t in range(NT):
            xr = xrr.next()
            dma("sp", xr[:], x_d[t0 + t * 128:t0 + (t + 1) * 128, :], writes=[xr])
            xrs.append(xr)
        def onorm_tail(h, o16):
            ofk = [(of32, h // 4, p) for p in range(NT)]
            pss = pm.next()
            mm(pss[:, 0:NB], ones[:], o16[:], True, True, [ones, o16], [pss])
            ro = ror.next()
            act(ro[:], pss[:, 0:NB], AF.Ln, [pss, epst], [ro], scale=1.0 / 128, bias=epst[:, 0:1])
            act(ro[:], ro[:], AF.Exp, [ro], [ro], scale=-0.5)
            tt("dve", of32[:, h, :], of32[:, h, :], ro[:], ALU.mult, ofk + [ro], ofk)

        pend = None
        wgs = {}
        for h in range(H):
            if h % 4 == 0:
                wgs[h // 4] = wget()
            pg = pm.next()
            mm_fm(pg[:, 0:NB], pg, wgs[h // 4], h % 4, hT, NB, hTk)
            act(graw[:, h, :], pg[:, 0:NB], AF.Identity, [pg], [(graw, h)])
            ofk = [(of32, h // 4, p) for p in range(NT)]
            o16 = q16r.next()
            act(o16[:], of32[:, h, :], AF.Square, ofk, [o16])
            if pend is not None:
                onorm_tail(*pend)
            pend = (h, o16)
        onorm_tail(*pend)
        for h in range(H):
            ofk = [(of32, h // 4, p) for p in range(NT)]
            act(graw[:, h, :], graw[:, h, :], AF.Silu, [(graw, h)], [(graw, h)])
            stt("dve", og[:, h, :], of32[:, h, :], gng[:, h:h + 1], graw[:, h, :], ALU.mult, ALU.mult,
                ofk + [vecs, (graw, h)], [(og, h)])
        if n + 1 < NBLK:
            for c_ in range(KC):
                ln_norm(c_)
        ogk = [(og, h) for h in range(H)]
        for uo in range(2):
            wm = wget()
            wr = wget()
            sms = []
            for j in range(4):
                pmr = pm.next()
                mm_fm(pmr[:, 0:NB], pmr, wm, j, hT, NB, hTk)
                sm = sgr.next()
                act(sm[:], pmr[:, 0:NB], AF.Sigmoid, [pmr], [sm])
                sms.append(sm)
            for j in range(4):
                m = 4 * uo + j
                sm = sms[j]
                py = pm.next()
                for k in range(KC):
                    mm(py[:, 0:NB], wr[:, k, j * 128:(j + 1) * 128], og[:, k, :], k == 0, k == KC - 1, [wr] + ogk, [py])
                tt("dve", sm[:], py[:, 0:NB], sm[:], ALU.mult, [py, sm], [sm])
                tt("dve", merged[:, m, :], sm[:], ycg[:, m, :], ALU.add, [sm, (ycg, m)], [(merged, m)])
        mk = [(merged, m) for m in range(KC)]
        wo2 = [wget(), wget()]
        for t in range(NT):
            xr = xrs[t]
            for half in range(2):
                pp = pm.next()
                for k in range(KC):
                    mm(pp[:], merged[:, k, t * 128:(t + 1) * 128], wo2[half][:, k, :], k == 0, k == KC - 1,
                       mk + [wo2[half]], [pp])
                tt("dve", xr[:, half * 512:(half + 1) * 512], pp[:], xr[:, half * 512:(half + 1) * 512], ALU.add,
                   [pp, xr], [xr])
            ss = small.next()
            act(junk[:], xr[:], AF.Square, [xr], [junk, (ss, 0)], accum=ss[:, 0:1])
            act(ss[:, 1:2], ss[:, 0:1], AF.Ln, [(ss, 0), epst], [(ss, 1)], scale=1.0 / D, bias=epst[:, 0:1])
            act(ss[:, 2:3], ss[:, 1:2], AF.Exp, [(ss, 1)], [(ss, 2)], scale=-0.5)
            stt("dve", xr[:], xr[:], ss[:, 2:3], fgbc[:], ALU.mult, ALU.mult, [xr, (ss, 2), fgbc], [xr])
            finals.append(dma("sp", out_d[t0 + t * 128:t0 + (t + 1) * 128, :], xr[:], reads=[xr]))

    def interleave(ga, na, gb, nb):
        ia = ib = 0
        da = db = False
        while not (da and db):
            if not da and (db or ia * nb <= ib * na):
                try:
                    next(ga)
                    ia += 1
                except StopIteration:
                    da = True
            else:
                try:
                    next(gb)
                    ib += 1
                except StopIteration:
                    db = True

    def interleave_at(ga, gb, pos):
        for i, _ in enumerate(ga):
            if i in pos:
                next(gb, None)
        for _ in gb:
            pass

    for _ in C1(0):
        pass
    for c_ in range(KC):
        ln_norm(c_)
    for n in range(NBLK):
        C2R1(n)
        if n + 1 < NBLK:
            gc = C1(n + 1)
            next(gc)
            interleave_at(R2(n), gc, (0, 1, 2, 3, 5, 6, 7, 8))
        else:
            for _ in R2(n):
                pass
        R3(n)
    counts = S.emit(final_waits=finals)
    return nc, counts


SEQ_FULL = 4096
NB_FULL = 256
_CACHE = {}


def kernel(**inputs):
    x = np.asarray(inputs["x"], np.float32)
    ncore = x.shape[0]
    lay = host_layout(inputs)
    if "nc" not in _CACHE:
        _CACHE["nc"] = build_program(SEQ_FULL, NB_FULL)[0]
    nc = _CACHE["nc"]
    in_maps = []
    for c in range(ncore):
        im = dict(lay)
        im["x"] = np.ascontiguousarray(x[c])
        in_maps.append(im)
    res = run_bass_kernel_spmd(nc, in_maps, core_ids=list(range(ncore)))
    return np.stack([np.asarray(r["out"], np.float32) for r in res.results], axis=0)
```

```python
import numpy as np
import concourse.bass as bass
import concourse.mybir as mybir
from concourse.bass_utils import run_bass_kernel_spmd

F32 = mybir.dt.float32
BF16 = mybir.dt.bfloat16
AF = mybir.ActivationFunctionType
ALU = mybir.AluOpType

ENGS = ("pe", "act", "dve", "pool", "sp")


class Buf:
    _n = 0

    def __init__(self, t, name=None):
        self.t = t
        Buf._n += 1
        self.key = (name or "buf", Buf._n)

    def __getitem__(self, k):
        return self.t[k]


class Sched:
    def __init__(self, nc, n_dma_sems=8):
        self.nc = nc
        self.eng = dict(pe=nc.tensor, act=nc.scalar, dve=nc.vector, pool=nc.gpsimd, sp=nc.sync)
        self.ops = {e: [] for e in ENGS}
        self.last_w = {}
        self.readers = {}
        self.n_dma_sems = n_dma_sems
        self.dma_count = {e: 0 for e in ENGS}
        self.dma_sem_uses = {}
        self.dma_last = {}

    @staticmethod
    def _k(b):
        if isinstance(b, Buf):
            return b.key
        if isinstance(b, tuple) and len(b) > 0 and isinstance(b[0], Buf):
            return (b[0].key,) + tuple(b[1:])
        return b

    def op(self, e, fn, reads=(), writes=(), dma=False):
        idx = len(self.ops[e])
        deps = []
        rk = [self._k(b) for b in reads]
        wk = [self._k(b) for b in writes]
        for k in rk:
            lw = self.last_w.get(k)
            if lw is not None:
                deps.append(lw)
        for k in wk:
            lw = self.last_w.get(k)
            if lw is not None:
                deps.append(lw)
            deps.extend(self.readers.get(k, ()))
        rec = dict(eng=e, idx=idx, fn=fn, deps=deps, dma=dma, signal=False)
        if dma:
            slot = self.dma_count[e] % self.n_dma_sems
            self.dma_count[e] += 1
            n = self.dma_sem_uses.get((e, slot), 0) + 1
            self.dma_sem_uses[(e, slot)] = n
            rec["dslot"] = (e, slot)
            rec["dval"] = 16 * n
            prev = self.dma_last.get((e, slot))
            if prev is not None:
                deps.append(prev)
            tok = ("dma", e, slot, 16 * n)
            self.dma_last[(e, slot)] = tok
        else:
            tok = ("eng", e, idx)
        rec["tok"] = tok
        self.ops[e].append(rec)
        for k in rk:
            self.readers.setdefault(k, []).append(tok)
        for k in wk:
            self.last_w[k] = tok
            self.readers[k] = []
        return tok

    def emit(self, final_waits=()):
        nc = self.nc
        need = {e: [] for e in ENGS}
        for e in ENGS:
            seen = {}
            for rec in self.ops[e]:
                w = {}
                for d in rec["deps"]:
                    if d[0] == "eng":
                        _, pe_, pidx = d
                        if pe_ == e:
                            if e in ("pe", "sp"):
                                continue
                            if pidx >= rec["idx"]:
                                continue
                        key = ("eng", pe_)
                        val = pidx
                    else:
                        _, qe, slot, v = d
                        key = ("dma", qe, slot)
                        val = v
                    if seen.get(key, -1) >= val:
                        continue
                    if w.get(key, -1) < val:
                        w[key] = val
                for key, val in w.items():
                    seen[key] = val
                    if key[0] == "eng":
                        self.ops[key[1]][val]["signal"] = True
                rec["waits"] = w
        for e in ENGS:
            c = 0
            for rec in self.ops[e]:
                if rec["signal"]:
                    c += 1
                    rec["sigval"] = c
        esem = {e: nc.alloc_semaphore("s_" + e) for e in ENGS}
        dsem = {}
        for (e, slot) in self.dma_sem_uses:
            dsem[(e, slot)] = nc.alloc_semaphore("d_%s%d" % (e, slot))
        for e in ENGS:
            eh = self.eng[e]
            for rec in self.ops[e]:
                for key, val in rec["waits"].items():
                    if key[0] == "eng":
                        sv = self.ops[key[1]][val]["sigval"]
                        eh.wait_ge(esem[key[1]], sv)
                    else:
                        eh.wait_ge(dsem[(key[1], key[2])], val)
                ins = rec["fn"](eh)
                if rec["dma"]:
                    ins.then_inc(dsem[rec["dslot"]], 16)
                elif rec["signal"]:
                    ins.then_inc(esem[e], 1)
        eh = self.eng["sp"]
        for tok in final_waits:
            _, qe, slot, v = tok
            eh.wait_ge(dsem[(qe, slot)], v)
        n = {e: len(self.ops[e]) for e in ENGS}
        return n


D = 1024
KC = 8
H = 8
NMETA = 16
CW = 31
EPS = 1e-6
NUNIT = 24

_SEG = dict(glu_a=0, glu_b=1024, z=2048, q=3072, f=4096, i=5120, g=6144, mc=7168, mr=8192,
            wco=9216, wro=10240, wo=11264)
_UNIT_COLS = [
    ("glu_b", 0), ("glu_a", 0), ("glu_b", 1), ("glu_a", 1),
    ("z", 0), ("z", 1),
    ("mc", 0), ("wco", 0), ("mc", 1), ("wco", 1),
    ("q", 0), ("q", 1), ("f", 0), ("f", 1),
    ("i", 0), ("i", 1), ("g", 0), ("g", 1),
    ("mr", 0), ("wro", 0), ("mr", 1), ("wro", 1),
    ("wo", 0), ("wo", 1),
]


def host_layout(inputs):
    f32 = np.float32
    w_in = np.asarray(inputs["w_in"], f32)[0]
    wcat = np.concatenate([w_in, np.asarray(inputs["w_conv_out"], f32)[0],
                           np.asarray(inputs["w_rec_out"], f32)[0], np.asarray(inputs["w_out"], f32)[0]], axis=1)
    wall = np.empty((NUNIT, 128, KC, 512), f32)
    for u, (seg, half) in enumerate(_UNIT_COLS):
        c0 = _SEG[seg] + 512 * half
        blk = wcat[:, c0:c0 + 512]
        wall[u] = blk.reshape(KC, 128, 512).transpose(1, 0, 2)
    wall = wall.reshape(NUNIT, 128, KC * 512)
    fm = lambda v: np.ascontiguousarray(np.asarray(v, f32).reshape(KC, 128).T)
    lbl = np.asarray(inputs["lb_logits"], f32)
    vecs = np.concatenate([fm(inputs["norm_g"][0]), fm(inputs["conv_b"][0]), fm(inputs["ln_g"][0]),
                           fm(inputs["ln_b"][0]), fm(inputs["gnorm_g"][0]), fm(lbl[0]), fm(lbl[1])], axis=1)
    cw = np.asarray(inputs["conv_w"], f32)[0]
    convw = np.ascontiguousarray(cw.reshape(CW, KC, 128).transpose(2, 1, 0)).reshape(128, KC * CW)
    gbc = np.ascontiguousarray(np.broadcast_to(np.asarray(inputs["norm_g"], f32)[0][None, :], (128, D)))
    fgbc = np.ascontiguousarray(np.broadcast_to(np.asarray(inputs["final_g"], f32)[None, :], (128, D)))
    return dict(wall=wall, vecs=np.ascontiguousarray(vecs), convw=convw, gbc=gbc, fgbc=fgbc,
                meta=np.ascontiguousarray(np.asarray(inputs["meta_tokens"], f32)))


class Ring:
    def __init__(self, mk, name, shape, dt, n):
        self.bufs = [mk("%s%d" % (name, i), shape, dt) for i in range(n)]
        self.i = 0

    def next(self):
        b = self.bufs[self.i % len(self.bufs)]
        self.i += 1
        return b


def build_program(SEQ, NB, n_dma_sems=8):
    nc = bass.Bass("TRN2", target_bir_lowering=False)
    S = Sched(nc, n_dma_sems=n_dma_sems)
    NT = NB // 128
    NCH = NB // 64
    NBLK = SEQ // NB
    assert NB % 128 == 0 and SEQ % NB == 0

    def dram(name, shape, dt, kind):
        return nc.dram_tensor(name, list(shape), dt, kind=kind).ap()
    x_d = dram("x", [SEQ, D], F32, "ExternalInput")
    meta_d = dram("meta", [NMETA, D], F32, "ExternalInput")
    wall_d = dram("wall", [NUNIT, 128, KC * 512], F32, "ExternalInput")
    vecs_d = dram("vecs", [128, 56], F32, "ExternalInput")
    convw_d = dram("convw", [128, KC * CW], F32, "ExternalInput")
    gbc_d = dram("gbc", [128, D], F32, "ExternalInput")
    fgbc_d = dram("fgbc", [128, D], F32, "ExternalInput")
    out_d = dram("out", [SEQ, D], F32, "ExternalOutput")
    wscr_d = nc.dram_tensor("wscr", [NUNIT, 128, KC * 512], BF16, kind="Internal").ap()
    dscr_d = nc.dram_tensor("dscr", [KC, 128, CW * 128], BF16, kind="Internal").ap()

    def sb(name, shape, dt):
        return Buf(nc.alloc_sbuf_tensor("s_" + name, list(shape), dt), name)

    def psb(name, shape, dt=F32):
        return Buf(nc.alloc_psum_tensor("p_" + name, list(shape), dt), name)

    def dma(eng, out, in_, reads=(), writes=()):
        return S.op(eng, lambda e: e.dma_start(out=out, in_=in_), reads, writes, dma=True)

    def act(out, in_, func, reads, writes, scale=1.0, bias=None, accum=None):
        kw = dict(out=out, in_=in_, func=func, scale=scale)
        if bias is not None:
            kw["bias"] = bias
        if accum is not None:
            kw["accum_out"] = accum
        return S.op("act", lambda e: e.activation(**kw), reads, writes)

    def tt(eng, out, in0, in1, op, reads, writes):
        return S.op(eng, lambda e: e.tensor_tensor(out=out, in0=in0, in1=in1, op=op), reads, writes)

    def ts(eng, out, in0, s1, s2, op0, op1, reads, writes):
        if s2 is None:
            return S.op(eng, lambda e: e.tensor_scalar(out=out, in0=in0, scalar1=s1, scalar2=None, op0=op0), reads, writes)
        return S.op(eng, lambda e: e.tensor_scalar(out=out, in0=in0, scalar1=s1, scalar2=s2, op0=op0, op1=op1), reads, writes)

    def stt(eng, out, in0, scalar, in1, op0, op1, reads, writes):
        return S.op(eng, lambda e: e.scalar_tensor_tensor(out=out, in0=in0, scalar=scalar, in1=in1, op0=op0, op1=op1),
                    reads, writes)

    def cp(eng, out, in_, reads, writes):
        if eng == "act":
            return S.op("act", lambda e: e.copy(out=out, in_=in_), reads, writes)
        return S.op(eng, lambda e: e.tensor_copy(out=out, in_=in_), reads, writes)

    def mm(out, lhsT, rhs, start, stop, reads, writes, skip=False):
        if skip:
            return S.op("pe", lambda e: e.matmul(out, lhsT=lhsT, rhs=rhs, start=start, stop=stop,
                                                 skip_group_check=True), reads, writes)
        return S.op("pe", lambda e: e.matmul(out, lhsT=lhsT, rhs=rhs, start=start, stop=stop), reads, writes)

    def tr(out, in_, ident_ap, reads, writes):
        return S.op("pe", lambda e: e.transpose(out=out, in_=in_, identity=ident_ap), reads, writes)

    def memset(eng, ap, val, reads, writes):
        return S.op(eng, lambda e: e.memset(ap, val), reads, writes)

    ident32 = sb("ident32", [128, 128], F32)
    ident = sb("ident", [128, 128], BF16)
    ones = sb("ones", [128, 128], BF16)
    pmask = sb("pmask", [128, 128], F32)
    smask = sb("smask", [128, NB], F32)
    epst = sb("epst", [128, 1], F32)
    vecs = sb("vecs", [128, 56], F32)
    lbt = sb("lbt", [128, 32], F32)
    convw = sb("convw", [128, KC, CW], F32)
    convw16 = sb("convw16", [128, KC, CW], BF16)
    gbc = sb("gbc", [128, D], F32)
    fgbc = sb("fgbc", [128, D], F32)
    wring = Ring(sb, "wb", [128, KC, 512], BF16, 5)
    xin = Ring(sb, "xin", [128, D], F32, 2)
    junk = sb("junk", [128, D], BF16)
    small = Ring(sb, "small", [128, 4], F32, 4)
    xhr = Ring(sb, "xh", [128, D], BF16, 2)
    hTs = [sb("hT%d" % i, [128, KC, NB], BF16) for i in range(2)]
    hTm = sb("hTm", [128, KC, NMETA], BF16)
    aT = sb("aT", [128, KC, 30 + NB], BF16)
    halo = sb("halo", [128, KC, 30], BF16)
    sgr = Ring(sb, "sg", [128, NB], F32, 4)
    dgr = Ring(sb, "dg", [128, CW, 128], BF16, 2)
    cfull = sb("cfull", [128, KC, NB], F32)
    c16r = Ring(sb, "c16", [128, NB], BF16, 2)
    q16r = Ring(sb, "q16", [128, NB], BF16, 2)
    stat = [sb("stat%d" % i, [128, NB], F32) for i in range(4)]
    gated = sb("gated", [128, KC, NB], BF16)
    ycg = sb("ycg", [128, KC, NB], BF16)
    qf = sb("qf", [128, H, NB], F32)
    t1r = Ring(sb, "t1", [128, NB], F32, 4)
    t2r = Ring(sb, "t2", [128, NB], F32, 4)
    t3r = Ring(sb, "t3", [128, NB], F32, 4)
    t4r = Ring(sb, "t4", [128, NB], F32, 4)
    khall = sb("khall", [128, H, NB], BF16)
    ebl = sb("ebl", [128, H, NCH, 1], F32)
    qtT = sb("qtT", [128, H, NB], BF16)
    ktT = sb("ktT", [128, H, NB], BF16)
    khat = sb("khat", [128, NT, H, 128], BF16)
    vtm = sb("vtm", [128, NT, D], BF16)
    khm = sb("khm", [128, H, NMETA], BF16)
    khatm = sb("khatm", [NMETA, H, 128], BF16)
    vm = sb("vm", [NMETA, D], BF16)
    atr = Ring(sb, "at", [128, H, 128], BF16, 2)
    Sf = sb("Sf", [128, H, 128], F32)
    Sb = [sb("Sb%d" % i, [128, H, 128], BF16) for i in range(2)]
    of32 = sb("of32", [128, H, NB], F32)
    graw = sb("graw", [128, H, NB], F32)
    ror = Ring(sb, "ro", [128, NB], F32, 2)
    og = sb("og", [128, KC, NB], BF16)
    merged = sb("merged", [128, KC, NB], BF16)
    xrr = xin
    pm = Ring(psb, "pm", [128, 512], F32, 4)
    pd = [psb("pd%d" % i, [128, 512], F32) for i in range(2)]
    pst = psb("pst", [128, 512], F32)
    ptr = Ring(psb, "pt", [128, KC, 128], BF16, 1)

    ng, cb, lng, lnb, gng = (vecs[:, 8 * i:8 * i + 8] for i in range(5))
    lb, oml, noml = lbt[:, 8:16], lbt[:, 16:24], lbt[:, 24:32]

    dma("sp", vecs[:], vecs_d, writes=[vecs])
    dma("sp", convw[:], convw_d.rearrange("p (c j) -> p c j", c=KC), writes=[convw])
    dma("sp", gbc[:], gbc_d, writes=[gbc])
    dma("sp", fgbc[:], fgbc_d, writes=[fgbc])
    memset("pool", ident32[:], 0.0, [], [ident32])
    S.op("pool", lambda e: e.affine_select(out=ident32[:], in_=ident32[:], pattern=[[-1, 128]], compare_op=ALU.not_equal,
                                            fill=1.0, base=0, channel_multiplier=1), [ident32], [ident32])
    cp("pool", ident[:], ident32[:], [ident32], [ident])
    memset("pool", ones[:], 1.0, [], [ones])
    memset("pool", pmask[:], 1.0, [], [pmask])
    S.op("pool", lambda e: e.affine_select(out=pmask[:], in_=pmask[:], pattern=[[1, 128]], compare_op=ALU.is_ge,
                                            fill=0.0, base=0, channel_multiplier=-1), [pmask], [pmask])
    memset("pool", pmask[0:64, 64:128], 0.0, [pmask], [pmask])
    memset("pool", smask[:], 1.0, [], [smask])
    memset("pool", smask[:].rearrange("p (c j) -> p c j", j=64)[:, :, 0:1], 0.0, [smask], [smask])
    memset("pool", epst[:], EPS, [], [epst])
    memset("pool", halo[:], 0.0, [], [halo])
    tt("dve", lbt[:, 0:8], vecs[:, 40:48], vecs[:, 48:56], ALU.subtract, [vecs], [(lbt, 0)])
    act(lbt[:, 8:16], lbt[:, 0:8], AF.Sigmoid, [(lbt, 0)], [(lbt, 1)])
    act(lbt[:, 16:24], lbt[:, 0:8], AF.Sigmoid, [(lbt, 0)], [(lbt, 2)], scale=-1.0)
    ts("dve", lbt[:, 24:32], lbt[:, 16:24], -1.0, None, ALU.mult, None, [(lbt, 2)], [(lbt, 3)])
    LBK = [(lbt, 1), (lbt, 2), (lbt, 3)]
    cp("dve", convw16[:], convw[:], [convw], [convw16])

    for c_ in range(KC):
        dg_ = dgr.next()
        tt("dve", dg_[:], ident[:].unsqueeze(1).to_broadcast([128, CW, 128]),
           convw16[:, c_, :].unsqueeze(2).to_broadcast([128, CW, 128]), ALU.mult, [ident, convw16], [dg_])
        dma("sp", dscr_d[c_].rearrange("p (j m) -> p j m", j=CW), dg_[:], reads=[dg_], writes=[("dscr", c_)])

    seq = [0, 1, 2, 3, 12, 13, 14, 15] + [0, 1, 2, 3]
    for n_ in range(NBLK):
        seq += list(range(4, 16)) + ([0, 1, 2, 3] if n_ + 1 < NBLK else []) + list(range(16, NUNIT))
    converted = set()
    cast_i = [0]
    state = dict(issued=0, taken=0, bufs={})

    def convert(u):
        if u in converted:
            return
        converted.add(u)
        dma("pool", wscr_d[u], wall_d[u], writes=[("wscr", u)])
    for u_ in seq[:8 + NUNIT]:
        convert(u_)

    def issue_upto(i):
        while state["issued"] <= i and state["issued"] < len(seq):
            u = seq[state["issued"]]
            convert(u)
            b = wring.next()
            dma("sp", b[:], wscr_d[u].rearrange("p (k n) -> p k n", k=KC),
                reads=[("wscr", u)], writes=[b])
            state["bufs"][state["issued"]] = b
            state["issued"] += 1

    def wget():
        i = state["taken"]
        issue_upto(i + 3)
        state["taken"] += 1
        return state["bufs"].pop(i)

    def load_norm_transpose(src, ntok, dst, col0, key):
        norm_transpose_b(norm_transpose_a(src, ntok), ntok, dst, col0, key)

    def norm_transpose_a(src, ntok):
        xt = xin.next()
        dma("sp", xt[0:ntok, :], src, writes=[xt])
        ss = small.next()
        act(junk[0:ntok, :], xt[0:ntok, :], AF.Square, [xt], [junk, (ss, 0)], accum=ss[0:ntok, 0:1])
        act(ss[0:ntok, 1:2], ss[0:ntok, 0:1], AF.Ln, [(ss, 0), epst], [(ss, 1)], scale=1.0 / D, bias=epst[0:ntok, 0:1])
        act(ss[0:ntok, 2:3], ss[0:ntok, 1:2], AF.Exp, [(ss, 1)], [(ss, 2)], scale=-0.5)
        xh = xhr.next()
        stt("dve", xh[0:ntok, :], xt[0:ntok, :], ss[0:ntok, 2:3], gbc[0:ntok, :], ALU.mult, ALU.mult,
            [xt, (ss, 2), gbc], [xh])
        return xh

    def norm_transpose_b(xh, ntok, dst, col0, key):
        pt = ptr.next()
        for k in range(KC):
            tr(pt[:, k, 0:ntok], xh[0:ntok, k * 128:(k + 1) * 128], ident[0:ntok, 0:ntok], [xh, ident], [pt])
        cp("dve", dst[:, :, col0:col0 + ntok], pt[:, :, 0:ntok], [pt], [key])

    def mm_fm(ps_ap, psbuf, w, j, actT, ntok, akeys):
        for k in range(KC):
            mm(ps_ap, w[:, k, j * 128:(j + 1) * 128], actT[:, k, 0:ntok], k == 0, k == KC - 1, [w] + akeys, [psbuf])

    def glu_stage(actT, ntok, akeys, dst_fn, dst_key_fn):
        for _ in glu_steps(actT, ntok, akeys, dst_fn, dst_key_fn):
            pass

    def glu_steps(actT, ntok, akeys, dst_fn, dst_key_fn):
        for uc in range(2):
            wb = wget()
            wa = wget()
            for j in range(4):
                c = 4 * uc + j
                pb = pm.next()
                mm_fm(pb[:, 0:ntok], pb, wb, j, actT, ntok, akeys)
                sg = sgr.next()
                act(sg[:, 0:ntok], pb[:, 0:ntok], AF.Sigmoid, [pb], [sg])
                pa = pm.next()
                mm_fm(pa[:, 0:ntok], pa, wa, j, actT, ntok, akeys)
                tt("dve", dst_fn(c), pa[:, 0:ntok], sg[:, 0:ntok], ALU.mult, [pa, sg], [dst_key_fn(c)])
                yield

    def f_head(pf, ntok, h, full, sig=None, sigk=None):
        t1, t2, t3, t4 = t1r.next(), t2r.next(), t3r.next(), t4r.next()
        if sig is None:
            act(t1[:, 0:ntok], pf[:, 0:ntok], AF.Sigmoid, [pf], [t1])
            sig, sigk = t1[:, 0:ntok], [t1]
        ts("dve", t2[:, 0:ntok], sig, noml[:, h:h + 1], oml[:, h:h + 1], ALU.mult, ALU.add, sigk + LBK, [t2])
        act(t1[:, 0:ntok], sig, AF.Ln, sigk + LBK, [t1], scale=oml[:, h:h + 1], bias=lb[:, h:h + 1])
        S.op("dve", lambda e: e.tensor_tensor_scan(out=t3[:, 0:ntok], data0=smask[:, 0:ntok], data1=t1[:, 0:ntok],
                                                   initial=0.0, op0=ALU.mult, op1=ALU.add), [smask, t1], [t3])
        cl = min(64, ntok)
        t3v = t3[:, 0:ntok].rearrange("p (c j) -> p c j", j=cl)
        t4v = t4[:, 0:ntok].rearrange("p (c j) -> p c j", j=cl)
        nch = ntok // cl
        if full:
            act(t1[:, 0:ntok], t3[:, 0:ntok], AF.Exp, [t3], [t1])
            t1v = t1[:, 0:ntok].rearrange("p (c j) -> p c j", j=cl)
            cp("pool", ebl[:, h, :, :], t1v[:, :, cl - 1:cl], [t1], [(ebl, h)])
            tt("dve", qtT[:, h, :], qf[:, h, :], t1[:, 0:ntok], ALU.mult, [(qf, h), t1], [(qtT, h)])
            act(t4[:, 0:ntok], t3[:, 0:ntok], AF.Exp, [t3], [t4], scale=-1.0)
            tt("dve", ktT[:, h, :], t2[:, 0:ntok], t4[:, 0:ntok], ALU.mult, [t2, t4], [(ktT, h)])
        tt("dve", t4v, t3v, t3v[:, :, cl - 1:cl].to_broadcast([128, nch, cl]), ALU.subtract, [t3, t4], [t4])
        act(t4[:, 0:ntok], t4[:, 0:ntok], AF.Exp, [t4], [t4], scale=-1.0)
        return t2, t4

    load_norm_transpose(meta_d, NMETA, hTm, 0, hTm)
    glu_stage(hTm, NMETA, [hTm], lambda c: halo[:, c, 30 - NMETA:30], lambda c: halo)
    ptm = ptr.next()
    for uf in range(2):
        wf = wget()
        for j in range(4):
            h = 4 * uf + j
            pf = pm.next()
            mm_fm(pf[:, 0:NMETA], pf, wf, j, hTm, NMETA, [hTm])
            act(of32[:, h, 0:NMETA], pf[:, 0:NMETA], AF.Sigmoid, [pf], [("msig", h)])
    for h in range(H):
        t2, t4 = f_head(None, NMETA, h, False, sig=of32[:, h, 0:NMETA], sigk=[("msig", h)])
        tt("dve", khm[:, h, :], t2[:, 0:NMETA], t4[:, 0:NMETA], ALU.mult, [t2, t4], [(khm, h)])
        tr(ptm[0:NMETA, h, :], khm[:, h, :], ident[:], [(khm, h), ident], [ptm])
    cp("dve", khatm[:], ptm[0:NMETA, :, :], [ptm], [khatm])
    wi = [wget(), wget()]
    for half in range(2):
        pv = pm.next()
        for k in range(KC):
            mm(pv[0:NMETA, :], hTm[:, k, :], wi[half][:, k, :], k == 0, k == KC - 1, [hTm, wi[half]], [pv])
        cp("act", vm[:, half * 512:(half + 1) * 512], pv[0:NMETA, :], [pv], [vm])
    pu = [pm.next(), pm.next()]
    for h in range(H):
        mm(pu[h // 4][:, (h % 4) * 128:(h % 4 + 1) * 128], khatm[:, h, :], vm[:, h * 128:(h + 1) * 128],
           True, True, [khatm, vm], [pu[h // 4]])
    cur = 0
    for g in range(2):
        cp("dve", Sf[:, 4 * g:4 * g + 4, :], pu[g][:].rearrange("p (g t) -> p g t", g=4), [pu[g]], [(Sf, g)])
        cp("act", Sb[cur][:, 4 * g:4 * g + 4, :], Sf[:, 4 * g:4 * g + 4, :], [(Sf, g)], [(Sb[cur], g)])

    finals = []
    aTk = [(aT, c) for c in range(KC)]
    hT = hTs[0]
    for t in range(NT):
        load_norm_transpose(x_d[t * 128:(t + 1) * 128, :], 128, hT, t * 128, (hT, t))
    assert 2 * NB <= 512
    sk = lambda h: [("sigf", h), ("msig", h)] + [(of32, h // 4, p) for p in range(NT)]
    v3 = lambda ap: ap.rearrange("p (c j) -> p c j", j=64)
    mu, msq, var, rstd = (s_[:] for s_ in stat)
    cur_box = [cur]

    def C1(n):
        hT = hTs[n % 2]
        hTk = [(hT, t) for t in range(NT)]
        cp("pool", aT[:, :, 0:30], halo[:], [halo], [(aT, "h")])
        glu_stage(hT, NB, hTk, lambda c: aT[:, c, 30:30 + NB], lambda c: (aT, c))
        yield
        dgs = {}

        def build_dg(c):
            dg = dgr.next()
            dma("sp", dg[:], dscr_d[c].rearrange("p (j m) -> p j m", j=CW), reads=[("dscr", c)], writes=[dg])
            dgs[c] = dg
        build_dg(0)
        memset("dve", pst[:], 0.0, [], [pst])
        prev = None
        for c in range(KC):
            if c + 1 < KC:
                build_dg(c + 1)
            dg = dgs.pop(c)
            pc = pm.next()
            for j in range(CW):
                mm(pc[:, 0:NB], dg[:, j, :], aT[:, c, j:j + NB], j == 0, j == CW - 1, [dg, (aT, c), (aT, "h")], [pc])
            act(cfull[:, c, :], pc[:, 0:NB], AF.Identity, [pc, vecs], [(cfull, c)], bias=cb[:, c:c + 1])
            c16 = c16r.next()
            q16 = q16r.next()
            act(q16[:], pc[:, 0:NB], AF.Square, [pc, vecs], [q16], bias=cb[:, c:c + 1])
            act(c16[:], pc[:, 0:NB], AF.Identity, [pc, vecs], [c16], bias=cb[:, c:c + 1])
            if prev is not None:
                pc_, c16_, q16_ = prev
                mm(pst[:, 0:NB], ones[:], c16_[:], False, False, [ones, c16_], [pst], skip=True)
                mm(pst[:, NB:2 * NB], ones[:], q16_[:], False, False, [ones, q16_], [pst], skip=True)
            prev = (c, c16, q16)
            yield
        pc_, c16_, q16_ = prev
        mm(pst[:, 0:NB], ones[:], c16_[:], False, True, [ones, c16_], [pst], skip=True)
        mm(pst[:, NB:2 * NB], ones[:], q16_[:], False, True, [ones, q16_], [pst], skip=True)
        cp("pool", halo[:], aT[:, :, NB:NB + 30], aTk, [halo])
        S.op("act", lambda e: e.mul(out=mu, in_=pst[:, 0:NB], mul=1.0 / D), [pst], [stat[0]])
        tt("dve", msq, mu, mu, ALU.mult, [stat[0]], [stat[1]])
        stt("dve", var, pst[:, NB:2 * NB], 1.0 / D, msq, ALU.mult, ALU.subtract, [pst, stat[1]], [stat[2]])
        act(var, var, AF.Ln, [stat[2], epst], [stat[2]], bias=epst[:, 0:1])
        act(rstd, var, AF.Exp, [stat[2]], [stat[3]], scale=-0.5)
        yield

    def ln_norm(c):
        tt("dve", cfull[:, c, :], cfull[:, c, :], mu, ALU.subtract, [(cfull, c), stat[0]], [(cfull, c)])
        tt("dve", cfull[:, c, :], cfull[:, c, :], rstd, ALU.mult, [(cfull, c), stat[3]], [(cfull, c)])
        act(cfull[:, c, :], cfull[:, c, :], AF.Silu, [(cfull, c), vecs], [(cfull, c)],
            scale=lng[:, c:c + 1], bias=lnb[:, c:c + 1])

    def C2R1(n):
        t0 = n * NB
        hT = hTs[n % 2]
        hTn = hTs[(n + 1) % 2]
        hTk = [(hT, t) for t in range(NT)]
        for uz in range(2):
            wz = wget()
            for j in range(4):
                c = 4 * uz + j
                pz = pm.next()
                mm_fm(pz[:, 0:NB], pz, wz, j, hT, NB, hTk)
                sz = sgr.next()
                act(sz[:], pz[:, 0:NB], AF.Silu, [pz], [sz])
                tt("dve", gated[:, c, :], cfull[:, c, :], sz[:], ALU.mult, [(cfull, c), sz], [(gated, c)])
        gk = [(gated, c) for c in range(KC)]
        for uo in range(2):
            wm = wget()
            wo = wget()
            sms = []
            for j in range(4):
                pmc = pm.next()
                mm_fm(pmc[:, 0:NB], pmc, wm, j, hT, NB, hTk)
                sm = sgr.next()
                act(sm[:], pmc[:, 0:NB], AF.Sigmoid, [pmc], [sm])
                sms.append(sm)
            for j in range(4):
                m = 4 * uo + j
                sm = sms[j]
                py = pm.next()
                for k in range(KC):
                    mm(py[:, 0:NB], wo[:, k, j * 128:(j + 1) * 128], gated[:, k, :], k == 0, k == KC - 1, [wo] + gk, [py])
                tt("dve", ycg[:, m, :], py[:, 0:NB], sm[:], ALU.mult, [py, sm], [(ycg, m)])
        xhn = []
        if n + 1 < NBLK:
            for t in range(NT):
                xhn.append(norm_transpose_a(x_d[t0 + NB + t * 128:t0 + NB + (t + 1) * 128, :], 128))
        for uq in range(2):
            wq = wget()
            for j in range(4):
                h = 4 * uq + j
                pq = pm.next()
                mm_fm(pq[:, 0:NB], pq, wq, j, hT, NB, hTk)
                act(qf[:, h, :], pq[:, 0:NB], AF.Silu, [pq], [(qf, h)])
        for uf in range(2):
            wf = wget()
            for j in range(4):
                h = 4 * uf + j
                pf = pm.next()
                mm_fm(pf[:, 0:NB], pf, wf, j, hT, NB, hTk)
                act(of32[:, h, :], pf[:, 0:NB], AF.Sigmoid, [pf], sk(h))
        wi = [wget(), wget()]
        for t in range(NT):
            for half in range(2):
                pv = pm.next()
                for k in range(KC):
                    mm(pv[:], hT[:, k, t * 128:(t + 1) * 128], wi[half][:, k, :], k == 0, k == KC - 1,
                       [(hT, t), wi[half]], [pv])
                cp("act", vtm[:, t, half * 512:(half + 1) * 512], pv[:], [pv], [(vtm, t, half)])
        for t, xh_ in enumerate(xhn):
            norm_transpose_b(xh_, 128, hTn, t * 128, (hTn, t))

    def R2(n):
        for g0 in (0, 4):
            G = range(g0, g0 + 4)
            T = {h: (t1r.next(), t2r.next(), t3r.next(), t4r.next()) for h in G}
            for h in G:
                t1, t2, t3, t4 = T[h]
                ts("dve", t2[:], of32[:, h, :], noml[:, h:h + 1], oml[:, h:h + 1], ALU.mult, ALU.add, sk(h) + LBK, [t2])
                act(t1[:], of32[:, h, :], AF.Ln, sk(h) + LBK, [t1], scale=oml[:, h:h + 1], bias=lb[:, h:h + 1])
            yield
            for h in G:
                t1, t2, t3, t4 = T[h]
                S.op("dve", (lambda t3=t3, t1=t1: lambda e: e.tensor_tensor_scan(
                    out=t3[:], data0=smask[:], data1=t1[:], initial=0.0, op0=ALU.mult, op1=ALU.add))(), [smask, t1], [t3])
            yield
            for h in G:
                t1, t2, t3, t4 = T[h]
                act(t1[:], t3[:], AF.Exp, [t3], [t1])
                cp("pool", ebl[:, h, :, :], v3(t1[:])[:, :, 63:64], [t1], [(ebl, h)])
                tt("dve", qtT[:, h, :], qf[:, h, :], t1[:], ALU.mult, [(qf, h), t1], [(qtT, h)])
            yield
            for h in G:
                t1, t2, t3, t4 = T[h]
                act(t4[:], t3[:], AF.Exp, [t3], [t4], scale=-1.0)
                tt("dve", ktT[:, h, :], t2[:], t4[:], ALU.mult, [t2, t4], [(ktT, h)])
            yield
            for h in G:
                t1, t2, t3, t4 = T[h]
                tt("dve", v3(t4[:]), v3(t3[:]), v3(t3[:])[:, :, 63:64].to_broadcast([128, NCH, 64]), ALU.subtract,
                   [t3, t4], [t4])
                act(t4[:], t4[:], AF.Exp, [t4], [t4], scale=-1.0)
                tt("dve", khall[:, h, :], t2[:], t4[:], ALU.mult, [t2, t4], [(khall, h)])
            yield
        assert NT * 4 <= KC
        for g in range(2):
            pt = ptr.next()
            for hh in range(4):
                h = 4 * g + hh
                for t in range(NT):
                    tr(pt[:, t * 4 + hh, :], khall[:, h, t * 128:(t + 1) * 128], ident[:], [(khall, h), ident], [pt])
            cp("act", khat[:, :, 4 * g:4 * g + 4, :], pt[:, 0:NT * 4, :].rearrange("p (t h) m -> p t h m", t=NT),
               [pt], [(khat, 4 * g + hh) for hh in range(4)])
            yield
        for p in range(NT):
            vk = [(vtm, p, 0), (vtm, p, 1)]
            pa = [pm.next(), pm.next()]
            for h in range(H):
                mm(pa[h // 4][:, (h % 4) * 128:(h % 4 + 1) * 128], ktT[:, h, p * 128:(p + 1) * 128],
                   qtT[:, h, p * 128:(p + 1) * 128], True, True, [(ktT, h), (qtT, h)], [pa[h // 4]])
            at = atr.next()
            for g in range(2):
                tt("dve", at[:, 4 * g:4 * g + 4, :], pa[g][:].rearrange("p (g t) -> p g t", g=4),
                   pmask[:].unsqueeze(1).to_broadcast([128, 4, 128]), ALU.mult, [pa[g], pmask], [(at, g)])
            yield
            pus = {}

            def u_mm(ci):
                r0 = 64 * ci
                pu = [pm.next(), pm.next()]
                for h in range(H):
                    mm(pu[h // 4][:, (h % 4) * 128:(h % 4 + 1) * 128], khat[r0:r0 + 64, p, h, :],
                       vtm[r0:r0 + 64, p, h * 128:(h + 1) * 128], True, True, vk + [(khat, h)], [pu[h // 4]])
                pus[ci] = pu

            def o_mm(ci):
                cur = cur_box[0]
                r0 = 64 * ci
                for h in range(H):
                    po = pd[h // 4]
                    c0 = (h % 4) * 128 + r0
                    mm(po[:, c0:c0 + 64], vtm[r0:r0 + 64, p, h * 128:(h + 1) * 128], at[r0:r0 + 64, h, r0:r0 + 64],
                       True, False, vk + [(at, h // 4)], [po])
                    mm(po[:, c0:c0 + 64], Sb[cur][:, h, :], qtT[:, h, p * 128 + r0:p * 128 + r0 + 64],
                       False, True, [(Sb[cur], h // 4), (qtT, h)], [po])

            def s_update(ci):
                cur = cur_box[0]
                gch = 2 * p + ci
                pu = pus[ci]
                nxt = 1 - cur
                for g in range(2):
                    hs = slice(4 * g, 4 * g + 4)
                    tt("dve", Sf[:, hs, :], Sf[:, hs, :], ebl[:, hs, gch, :].to_broadcast([128, 4, 128]), ALU.mult,
                       [(Sf, g)] + [(ebl, h) for h in range(4 * g, 4 * g + 4)], [(Sf, g)])
                    tt("dve", Sf[:, hs, :], Sf[:, hs, :], pu[g][:].rearrange("p (g t) -> p g t", g=4), ALU.add,
                       [(Sf, g), pu[g]], [(Sf, g)])
                    cp("act", Sb[nxt][:, hs, :], Sf[:, hs, :], [(Sf, g)], [(Sb[nxt], g)])
                cur_box[0] = nxt

            o_mm(0)
            yield
            u_mm(0)
            s_update(0)
            yield
            o_mm(1)
            yield
            u_mm(1)
            s_update(1)
            for g in range(2):
                cp("act", of32[:, 4 * g:4 * g + 4, p * 128:(p + 1) * 128], pd[g][:].rearrange("p (g t) -> p g t", g=4),
                   [pd[g]], [(of32, g, p)] + [("sigf", 4 * g + q) for q in range(4)])
            yield

    def R3(n):
        t0 = n * NB
        hT = hTs[n % 2]
        hTk = [(hT, t) for t in range(NT)]
        xrs = []
        for t in range(NT):
            xr = xrr.next()
            dma("sp", xr[:], x_d[t0 + t * 128:t0 + (t + 1) * 128, :], writes=[xr])
            xrs.append(xr)
        def onorm_tail(h, o16):
            ofk = [(of32, h // 4, p) for p in range(NT)]
            pss = pm.next()
            mm(pss[:, 0:NB], ones[:], o16[:], True, True, [ones, o16], [pss])
            ro = ror.next()
            act(ro[:], pss[:, 0:NB], AF.Ln, [pss, epst], [ro], scale=1.0 / 128, bias=epst[:, 0:1])
            act(ro[:], ro[:], AF.Exp, [ro], [ro], scale=-0.5)
            tt("dve", of32[:, h, :], of32[:, h, :], ro[:], ALU.mult, ofk + [ro], ofk)

        pend = None
        wgs = {}
        for h in range(H):
            if h % 4 == 0:
                wgs[h // 4] = wget()
            pg = pm.next()
            mm_fm(pg[:, 0:NB], pg, wgs[h // 4], h % 4, hT, NB, hTk)
            act(graw[:, h, :], pg[:, 0:NB], AF.Identity, [pg], [(graw, h)])
            ofk = [(of32, h // 4, p) for p in range(NT)]
            o16 = q16r.next()
            act(o16[:], of32[:, h, :], AF.Square, ofk, [o16])
            if pend is not None:
                onorm_tail(*pend)
            pend = (h, o16)
        onorm_tail(*pend)
        for h in range(H):
            ofk = [(of32, h // 4, p) for p in range(NT)]
            act(graw[:, h, :], graw[:, h, :], AF.Silu, [(graw, h)], [(graw, h)])
            stt("dve", og[:, h, :], of32[:, h, :], gng[:, h:h + 1], graw[:, h, :], ALU.mult, ALU.mult,
                ofk + [vecs, (graw, h)], [(og, h)])
        ogk = [(og, h) for h in range(H)]
        for uo in range(2):
            wm = wget()
            wr = wget()
            sms = []
            for j in range(4):
                pmr = pm.next()
                mm_fm(pmr[:, 0:NB], pmr, wm, j, hT, NB, hTk)
                sm = sgr.next()
                act(sm[:], pmr[:, 0:NB], AF.Sigmoid, [pmr], [sm])
                sms.append(sm)
            for j in range(4):
                m = 4 * uo + j
                sm = sms[j]
                py = pm.next()
                for k in range(KC):
                    mm(py[:, 0:NB], wr[:, k, j * 128:(j + 1) * 128], og[:, k, :], k == 0, k == KC - 1, [wr] + ogk, [py])
                tt("dve", sm[:], py[:, 0:NB], sm[:], ALU.mult, [py, sm], [sm])
                if n + 1 < NBLK and uo == 1:
                    ln_norm(2 * j)
                    ln_norm(2 * j + 1)
                tt("dve", merged[:, m, :], sm[:], ycg[:, m, :], ALU.add, [sm, (ycg, m)], [(merged, m)])
        mk = [(merged, m) for m in range(KC)]
        wo2 = [wget(), wget()]
        for t in range(NT):
            xr = xrs[t]
            for half in range(2):
                pp = pm.next()
                for k in range(KC):
                    mm(pp[:], merged[:, k, t * 128:(t + 1) * 128], wo2[half][:, k, :], k == 0, k == KC - 1,
                       mk + [wo2[half]], [pp])
                tt("dve", xr[:, half * 512:(half + 1) * 512], pp[:], xr[:, half * 512:(half + 1) * 512], ALU.add,
                   [pp, xr], [xr])
            ss = small.next()
            act(junk[:], xr[:], AF.Square, [xr], [junk, (ss, 0)], accum=ss[:, 0:1])
            act(ss[:, 1:2], ss[:, 0:1], AF.Ln, [(ss, 0), epst], [(ss, 1)], scale=1.0 / D, bias=epst[:, 0:1])
            act(ss[:, 2:3], ss[:, 1:2], AF.Exp, [(ss, 1)], [(ss, 2)], scale=-0.5)
            stt("dve", xr[:], xr[:], ss[:, 2:3], fgbc[:], ALU.mult, ALU.mult, [xr, (ss, 2), fgbc], [xr])
            finals.append(dma("sp", out_d[t0 + t * 128:t0 + (t + 1) * 128, :], xr[:], reads=[xr]))

    def interleave(ga, na, gb, nb):
        ia = ib = 0
        da = db = False
        while not (da and db):
            if not da and (db or ia * nb <= ib * na):
                try:
                    next(ga)
                    ia += 1
                except StopIteration:
                    da = True
            else:
                try:
                    next(gb)
                    ib += 1
                except StopIteration:
                    db = True

    def interleave_at(ga, gb, pos):
        for i, _ in enumerate(ga):
            if i in pos:
                next(gb, None)
        for _ in gb:
            pass

    for _ in C1(0):
        pass
    for c_ in range(KC):
        ln_norm(c_)
    for n in range(NBLK):
        C2R1(n)
        if n + 1 < NBLK:
            gc = C1(n + 1)
            next(gc)
            interleave_at(R2(n), gc, (0, 1, 2, 3, 5, 6, 7, 8))
        else:
            for _ in R2(n):
                pass
        R3(n)
    counts = S.emit(final_waits=finals)
    return nc, counts


SEQ_FULL = 4096
NB_FULL = 256
_CACHE = {}


def kernel(**inputs):
    x = np.asarray(inputs["x"], np.float32)
    ncore = x.shape[0]
    lay = host_layout(inputs)
    if "nc" not in _CACHE:
        _CACHE["nc"] = build_program(SEQ_FULL, NB_FULL)[0]
    nc = _CACHE["nc"]
    in_maps = []
    for c in range(ncore):
        im = dict(lay)
        im["x"] = np.ascontiguousarray(x[c])
        in_maps.append(im)
    res = run_bass_kernel_spmd(nc, in_maps, core_ids=list(range(ncore)))
    return np.stack([np.asarray(r["out"], np.float32) for r in res.results], axis=0)
```

```python
import numpy as np
import concourse.bass as bass
import concourse.mybir as mybir
from concourse.bass_utils import run_bass_kernel_spmd

F32 = mybir.dt.float32
BF16 = mybir.dt.bfloat16
AF = mybir.ActivationFunctionType
ALU = mybir.AluOpType

ENGS = ("pe", "act", "dve", "pool", "sp")


class Buf:
    _n = 0

    def __init__(self, t, name=None):
        self.t = t
        Buf._n += 1
        self.key = (name or "buf", Buf._n)

    def __getitem__(self, k):
        return self.t[k]


class Sched:
    def __init__(self, nc, n_dma_sems=8):
        self.nc = nc
        self.eng = dict(pe=nc.tensor, act=nc.scalar, dve=nc.vector, pool=nc.gpsimd, sp=nc.sync)
        self.ops = {e: [] for e in ENGS}
        self.last_w = {}
        self.readers = {}
        self.n_dma_sems = n_dma_sems
        self.dma_count = {e: 0 for e in ENGS}
        self.dma_sem_uses = {}
        self.dma_last = {}

    @staticmethod
    def _k(b):
        if isinstance(b, Buf):
            return b.key
        if isinstance(b, tuple) and len(b) > 0 and isinstance(b[0], Buf):
            return (b[0].key,) + tuple(b[1:])
        return b

    def op(self, e, fn, reads=(), writes=(), dma=False):
        idx = len(self.ops[e])
        deps = []
        rk = [self._k(b) for b in reads]
        wk = [self._k(b) for b in writes]
        for k in rk:
            lw = self.last_w.get(k)
            if lw is not None:
                deps.append(lw)
        for k in wk:
            lw = self.last_w.get(k)
            if lw is not None:
                deps.append(lw)
            deps.extend(self.readers.get(k, ()))
        rec = dict(eng=e, idx=idx, fn=fn, deps=deps, dma=dma, signal=False)
        if dma:
            slot = self.dma_count[e] % self.n_dma_sems
            self.dma_count[e] += 1
            n = self.dma_sem_uses.get((e, slot), 0) + 1
            self.dma_sem_uses[(e, slot)] = n
            rec["dslot"] = (e, slot)
            rec["dval"] = 16 * n
            prev = self.dma_last.get((e, slot))
            if prev is not None:
                deps.append(prev)
            tok = ("dma", e, slot, 16 * n)
            self.dma_last[(e, slot)] = tok
        else:
            tok = ("eng", e, idx)
        rec["tok"] = tok
        self.ops[e].append(rec)
        for k in rk:
            self.readers.setdefault(k, []).append(tok)
        for k in wk:
            self.last_w[k] = tok
            self.readers[k] = []
        return tok

    def emit(self, final_waits=()):
        nc = self.nc
        need = {e: [] for e in ENGS}
        for e in ENGS:
            seen = {}
            for rec in self.ops[e]:
                w = {}
                for d in rec["deps"]:
                    if d[0] == "eng":
                        _, pe_, pidx = d
                        if pe_ == e:
                            if e in ("pe", "sp"):
                                continue
                            if pidx >= rec["idx"]:
                                continue
                        key = ("eng", pe_)
                        val = pidx
                    else:
                        _, qe, slot, v = d
                        key = ("dma", qe, slot)
                        val = v
                    if seen.get(key, -1) >= val:
                        continue
                    if w.get(key, -1) < val:
                        w[key] = val
                for key, val in w.items():
                    seen[key] = val
                    if key[0] == "eng":
                        self.ops[key[1]][val]["signal"] = True
                rec["waits"] = w
        for e in ENGS:
            c = 0
            for rec in self.ops[e]:
                if rec["signal"]:
                    c += 1
                    rec["sigval"] = c
        esem = {e: nc.alloc_semaphore("s_" + e) for e in ENGS}
        dsem = {}
        for (e, slot) in self.dma_sem_uses:
            dsem[(e, slot)] = nc.alloc_semaphore("d_%s%d" % (e, slot))
        for e in ENGS:
            eh = self.eng[e]
            for rec in self.ops[e]:
                for key, val in rec["waits"].items():
                    if key[0] == "eng":
                        sv = self.ops[key[1]][val]["sigval"]
                        eh.wait_ge(esem[key[1]], sv)
                    else:
                        eh.wait_ge(dsem[(key[1], key[2])], val)
                ins = rec["fn"](eh)
                if rec["dma"]:
                    ins.then_inc(dsem[rec["dslot"]], 16)
                elif rec["signal"]:
                    ins.then_inc(esem[e], 1)
        eh = self.eng["sp"]
        for tok in final_waits:
            _, qe, slot, v = tok
            eh.wait_ge(dsem[(qe, slot)], v)
        n = {e: len(self.ops[e]) for e in ENGS}
        return n


D = 1024
KC = 8
H = 8
NMETA = 16
CW = 31
EPS = 1e-6
NUNIT = 24

_SEG = dict(glu_a=0, glu_b=1024, z=2048, q=3072, f=4096, i=5120, g=6144, mc=7168, mr=8192,
            wco=9216, wro=10240, wo=11264)
_UNIT_COLS = [
    ("glu_b", 0), ("glu_a", 0), ("glu_b", 1), ("glu_a", 1),
    ("z", 0), ("z", 1),
    ("mc", 0), ("wco", 0), ("mc", 1), ("wco", 1),
    ("q", 0), ("q", 1), ("f", 0), ("f", 1),
    ("i", 0), ("i", 1), ("g", 0), ("g", 1),
    ("mr", 0), ("wro", 0), ("mr", 1), ("wro", 1),
    ("wo", 0), ("wo", 1),
]


def host_layout(inputs):
    f32 = np.float32
    w_in = np.asarray(inputs["w_in"], f32)[0]
    wcat = np.concatenate([w_in, np.asarray(inputs["w_conv_out"], f32)[0],
                           np.asarray(inputs["w_rec_out"], f32)[0], np.asarray(inputs["w_out"], f32)[0]], axis=1)
    wall = np.empty((NUNIT, 128, KC, 512), f32)
    for u, (seg, half) in enumerate(_UNIT_COLS):
        c0 = _SEG[seg] + 512 * half
        blk = wcat[:, c0:c0 + 512]
        wall[u] = blk.reshape(KC, 128, 512).transpose(1, 0, 2)
    wall = wall.reshape(NUNIT, 128, KC * 512)
    fm = lambda v: np.ascontiguousarray(np.asarray(v, f32).reshape(KC, 128).T)
    lbl = np.asarray(inputs["lb_logits"], f32)
    vecs = np.concatenate([fm(inputs["norm_g"][0]), fm(inputs["conv_b"][0]), fm(inputs["ln_g"][0]),
                           fm(inputs["ln_b"][0]), fm(inputs["gnorm_g"][0]), fm(lbl[0]), fm(lbl[1])], axis=1)
    cw = np.asarray(inputs["conv_w"], f32)[0]
    convw = np.ascontiguousarray(cw.reshape(CW, KC, 128).transpose(2, 1, 0)).reshape(128, KC * CW)
    gbc = np.ascontiguousarray(np.broadcast_to(np.asarray(inputs["norm_g"], f32)[0][None, :], (128, D)))
    fgbc = np.ascontiguousarray(np.broadcast_to(np.asarray(inputs["final_g"], f32)[None, :], (128, D)))
    return dict(wall=wall, vecs=np.ascontiguousarray(vecs), convw=convw, gbc=gbc, fgbc=fgbc,
                meta=np.ascontiguousarray(np.asarray(inputs["meta_tokens"], f32)))


class Ring:
    def __init__(self, mk, name, shape, dt, n):
        self.bufs = [mk("%s%d" % (name, i), shape, dt) for i in range(n)]
        self.i = 0

    def next(self):
        b = self.bufs[self.i % len(self.bufs)]
        self.i += 1
        return b


def build_program(SEQ, NB, n_dma_sems=8):
    nc = bass.Bass("TRN2", target_bir_lowering=False)
    S = Sched(nc, n_dma_sems=n_dma_sems)
    NT = NB // 128
    NCH = NB // 64
    NBLK = SEQ // NB
    assert NB % 128 == 0 and SEQ % NB == 0

    def dram(name, shape, dt, kind):
        return nc.dram_tensor(name, list(shape), dt, kind=kind).ap()
    x_d = dram("x", [SEQ, D], F32, "ExternalInput")
    meta_d = dram("meta", [NMETA, D], F32, "ExternalInput")
    wall_d = dram("wall", [NUNIT, 128, KC * 512], F32, "ExternalInput")
    vecs_d = dram("vecs", [128, 56], F32, "ExternalInput")
    convw_d = dram("convw", [128, KC * CW], F32, "ExternalInput")
    gbc_d = dram("gbc", [128, D], F32, "ExternalInput")
    fgbc_d = dram("fgbc", [128, D], F32, "ExternalInput")
    out_d = dram("out", [SEQ, D], F32, "ExternalOutput")
    wscr_d = nc.dram_tensor("wscr", [NUNIT, 128, KC * 512], BF16, kind="Internal").ap()
    dscr_d = nc.dram_tensor("dscr", [KC, 128, CW * 128], BF16, kind="Internal").ap()

    def sb(name, shape, dt):
        return Buf(nc.alloc_sbuf_tensor("s_" + name, list(shape), dt), name)

    def psb(name, shape, dt=F32):
        return Buf(nc.alloc_psum_tensor("p_" + name, list(shape), dt), name)

    def dma(eng, out, in_, reads=(), writes=()):
        return S.op(eng, lambda e: e.dma_start(out=out, in_=in_), reads, writes, dma=True)

    def act(out, in_, func, reads, writes, scale=1.0, bias=None, accum=None):
        kw = dict(out=out, in_=in_, func=func, scale=scale)
        if bias is not None:
            kw["bias"] = bias
        if accum is not None:
            kw["accum_out"] = accum
        return S.op("act", lambda e: e.activation(**kw), reads, writes)

    def tt(eng, out, in0, in1, op, reads, writes):
        return S.op(eng, lambda e: e.tensor_tensor(out=out, in0=in0, in1=in1, op=op), reads, writes)

    def ts(eng, out, in0, s1, s2, op0, op1, reads, writes):
        if s2 is None:
            return S.op(eng, lambda e: e.tensor_scalar(out=out, in0=in0, scalar1=s1, scalar2=None, op0=op0), reads, writes)
        return S.op(eng, lambda e: e.tensor_scalar(out=out, in0=in0, scalar1=s1, scalar2=s2, op0=op0, op1=op1), reads, writes)

    def stt(eng, out, in0, scalar, in1, op0, op1, reads, writes):
        return S.op(eng, lambda e: e.scalar_tensor_tensor(out=out, in0=in0, scalar=scalar, in1=in1, op0=op0, op1=op1),
                    reads, writes)

    def cp(eng, out, in_, reads, writes):
        if eng == "act":
            return S.op("act", lambda e: e.copy(out=out, in_=in_), reads, writes)
        return S.op(eng, lambda e: e.tensor_copy(out=out, in_=in_), reads, writes)

    def mm(out, lhsT, rhs, start, stop, reads, writes, skip=False):
        if skip:
            return S.op("pe", lambda e: e.matmul(out, lhsT=lhsT, rhs=rhs, start=start, stop=stop,
                                                 skip_group_check=True), reads, writes)
        return S.op("pe", lambda e: e.matmul(out, lhsT=lhsT, rhs=rhs, start=start, stop=stop), reads, writes)

    def tr(out, in_, ident_ap, reads, writes):
        return S.op("pe", lambda e: e.transpose(out=out, in_=in_, identity=ident_ap), reads, writes)

    def memset(eng, ap, val, reads, writes):
        return S.op(eng, lambda e: e.memset(ap, val), reads, writes)

    ident32 = sb("ident32", [128, 128], F32)
    ident = sb("ident", [128, 128], BF16)
    ones = sb("ones", [128, 128], BF16)
    pmask = sb("pmask", [128, 128], F32)
    smask = sb("smask", [128, NB], F32)
    epst = sb("epst", [128, 1], F32)
    vecs = sb("vecs", [128, 56], F32)
    lbt = sb("lbt", [128, 32], F32)
    convw = sb("convw", [128, KC, CW], F32)
    convw16 = sb("convw16", [128, KC, CW], BF16)
    gbc = sb("gbc", [128, D], F32)
    fgbc = sb("fgbc", [128, D], F32)
    wring = Ring(sb, "wb", [128, KC, 512], BF16, 5)
    xin = Ring(sb, "xin", [128, D], F32, 2)
    junk = sb("junk", [128, D], BF16)
    small = Ring(sb, "small", [128, 4], F32, 4)
    xhr = Ring(sb, "xh", [128, D], BF16, 2)
    hTs = [sb("hT%d" % i, [128, KC, NB], BF16) for i in range(2)]
    hTm = sb("hTm", [128, KC, NMETA], BF16)
    aT = sb("aT", [128, KC, 30 + NB], BF16)
    halo = sb("halo", [128, KC, 30], BF16)
    sgr = Ring(sb, "sg", [128, NB], F32, 4)
    dgr = Ring(sb, "dg", [128, CW, 128], BF16, 2)
    cfull = sb("cfull", [128, KC, NB], F32)
    c16r = Ring(sb, "c16", [128, NB], BF16, 2)
    q16r = Ring(sb, "q16", [128, NB], BF16, 2)
    stat = [sb("stat%d" % i, [128, NB], F32) for i in range(4)]
    gated = sb("gated", [128, KC, NB], BF16)
    ycg = sb("ycg", [128, KC, NB], BF16)
    qf = sb("qf", [128, H, NB], F32)
    t1r = Ring(sb, "t1", [128, NB], F32, 4)
    t2r = Ring(sb, "t2", [128, NB], F32, 4)
    t3r = Ring(sb, "t3", [128, NB], F32, 4)
    t4r = Ring(sb, "t4", [128, NB], F32, 4)
    khall = sb("khall", [128, H, NB], BF16)
    ebl = sb("ebl", [128, H, NCH, 1], F32)
    qtT = sb("qtT", [128, H, NB], BF16)
    ktT = sb("ktT", [128, H, NB], BF16)
    khat = sb("khat", [128, NT, H, 128], BF16)
    vtm = sb("vtm", [128, NT, D], BF16)
    khm = sb("khm", [128, H, NMETA], BF16)
    khatm = sb("khatm", [NMETA, H, 128], BF16)
    vm = sb("vm", [NMETA, D], BF16)
    atr = Ring(sb, "at", [128, H, 128], BF16, 2)
    Sf = sb("Sf", [128, H, 128], F32)
    Sb = [sb("Sb%d" % i, [128, H, 128], BF16) for i in range(2)]
    of32 = sb("of32", [128, H, NB], F32)
    graw = sb("graw", [128, H, NB], F32)
    ror = Ring(sb, "ro", [128, NB], F32, 2)
    og = sb("og", [128, KC, NB], BF16)
    merged = sb("merged", [128, KC, NB], BF16)
    xrr = xin
    pm = Ring(psb, "pm", [128, 512], F32, 4)
    pd = [psb("pd%d" % i, [128, 512], F32) for i in range(2)]
    pst = psb("pst", [128, 512], F32)
    ptr = Ring(psb, "pt", [128, KC, 128], BF16, 1)

    ng, cb, lng, lnb, gng = (vecs[:, 8 * i:8 * i + 8] for i in range(5))
    lb, oml, noml = lbt[:, 8:16], lbt[:, 16:24], lbt[:, 24:32]

    dma("sp", vecs[:], vecs_d, writes=[vecs])
    dma("sp", convw[:], convw_d.rearrange("p (c j) -> p c j", c=KC), writes=[convw])
    dma("sp", gbc[:], gbc_d, writes=[gbc])
    dma("sp", fgbc[:], fgbc_d, writes=[fgbc])
    memset("pool", ident32[:], 0.0, [], [ident32])
    S.op("pool", lambda e: e.affine_select(out=ident32[:], in_=ident32[:], pattern=[[-1, 128]], compare_op=ALU.not_equal,
                                            fill=1.0, base=0, channel_multiplier=1), [ident32], [ident32])
    cp("pool", ident[:], ident32[:], [ident32], [ident])
    memset("pool", ones[:], 1.0, [], [ones])
    memset("pool", pmask[:], 1.0, [], [pmask])
    S.op("pool", lambda e: e.affine_select(out=pmask[:], in_=pmask[:], pattern=[[1, 128]], compare_op=ALU.is_ge,
                                            fill=0.0, base=0, channel_multiplier=-1), [pmask], [pmask])
    memset("pool", pmask[0:64, 64:128], 0.0, [pmask], [pmask])
    memset("pool", smask[:], 1.0, [], [smask])
    memset("pool", smask[:].rearrange("p (c j) -> p c j", j=64)[:, :, 0:1], 0.0, [smask], [smask])
    memset("pool", epst[:], EPS, [], [epst])
    memset("pool", halo[:], 0.0, [], [halo])
    tt("dve", lbt[:, 0:8], vecs[:, 40:48], vecs[:, 48:56], ALU.subtract, [vecs], [(lbt, 0)])
    act(lbt[:, 8:16], lbt[:, 0:8], AF.Sigmoid, [(lbt, 0)], [(lbt, 1)])
    act(lbt[:, 16:24], lbt[:, 0:8], AF.Sigmoid, [(lbt, 0)], [(lbt, 2)], scale=-1.0)
    ts("dve", lbt[:, 24:32], lbt[:, 16:24], -1.0, None, ALU.mult, None, [(lbt, 2)], [(lbt, 3)])
    LBK = [(lbt, 1), (lbt, 2), (lbt, 3)]
    cp("dve", convw16[:], convw[:], [convw], [convw16])

    for c_ in range(KC):
        dg_ = dgr.next()
        tt("dve", dg_[:], ident[:].unsqueeze(1).to_broadcast([128, CW, 128]),
           convw16[:, c_, :].unsqueeze(2).to_broadcast([128, CW, 128]), ALU.mult, [ident, convw16], [dg_])
        dma("sp", dscr_d[c_].rearrange("p (j m) -> p j m", j=CW), dg_[:], reads=[dg_], writes=[("dscr", c_)])

    seq = [0, 1, 2, 3, 12, 13, 14, 15] + [0, 1, 2, 3]
    for n_ in range(NBLK):
        seq += list(range(4, 16)) + ([0, 1, 2, 3] if n_ + 1 < NBLK else []) + list(range(16, NUNIT))
    converted = set()
    cast_i = [0]
    state = dict(issued=0, taken=0, bufs={})

    def convert(u):
        if u in converted:
            return
        converted.add(u)
        dma("pool", wscr_d[u], wall_d[u], writes=[("wscr", u)])
    for u_ in seq[:8 + NUNIT]:
        convert(u_)

    def issue_upto(i):
        while state["issued"] <= i and state["issued"] < len(seq):
            u = seq[state["issued"]]
            convert(u)
            b = wring.next()
            dma("sp", b[:], wscr_d[u].rearrange("p (k n) -> p k n", k=KC),
                reads=[("wscr", u)], writes=[b])
            state["bufs"][state["issued"]] = b
            state["issued"] += 1

    def wget():
        i = state["taken"]
        issue_upto(i + 3)
        state["taken"] += 1
        return state["bufs"].pop(i)

    def load_norm_transpose(src, ntok, dst, col0, key):
        norm_transpose_b(norm_transpose_a(src, ntok), ntok, dst, col0, key)

    def norm_transpose_a(src, ntok):
        xt = xin.next()
        dma("sp", xt[0:ntok, :], src, writes=[xt])
        ss = small.next()
        act(junk[0:ntok, :], xt[0:ntok, :], AF.Square, [xt], [junk, (ss, 0)], accum=ss[0:ntok, 0:1])
        act(ss[0:ntok, 1:2], ss[0:ntok, 0:1], AF.Ln, [(ss, 0), epst], [(ss, 1)], scale=1.0 / D, bias=epst[0:ntok, 0:1])
        act(ss[0:ntok, 2:3], ss[0:ntok, 1:2], AF.Exp, [(ss, 1)], [(ss, 2)], scale=-0.5)
        xh = xhr.next()
        stt("dve", xh[0:ntok, :], xt[0:ntok, :], ss[0:ntok, 2:3], gbc[0:ntok, :], ALU.mult, ALU.mult,
            [xt, (ss, 2), gbc], [xh])
        return xh

    def norm_transpose_b(xh, ntok, dst, col0, key):
        pt = ptr.next()
        for k in range(KC):
            tr(pt[:, k, 0:ntok], xh[0:ntok, k * 128:(k + 1) * 128], ident[0:ntok, 0:ntok], [xh, ident], [pt])
        cp("dve", dst[:, :, col0:col0 + ntok], pt[:, :, 0:ntok], [pt], [key])

    def mm_fm(ps_ap, psbuf, w, j, actT, ntok, akeys):
        for k in range(KC):
            mm(ps_ap, w[:, k, j * 128:(j + 1) * 128], actT[:, k, 0:ntok], k == 0, k == KC - 1, [w] + akeys, [psbuf])

    def glu_stage(actT, ntok, akeys, dst_fn, dst_key_fn):
        for _ in glu_steps(actT, ntok, akeys, dst_fn, dst_key_fn):
            pass

    def glu_steps(actT, ntok, akeys, dst_fn, dst_key_fn):
        for uc in range(2):
            wb = wget()
            wa = wget()
            for j in range(4):
                c = 4 * uc + j
                pb = pm.next()
                mm_fm(pb[:, 0:ntok], pb, wb, j, actT, ntok, akeys)
                sg = sgr.next()
                act(sg[:, 0:ntok], pb[:, 0:ntok], AF.Sigmoid, [pb], [sg])
                pa = pm.next()
                mm_fm(pa[:, 0:ntok], pa, wa, j, actT, ntok, akeys)
                tt("dve", dst_fn(c), pa[:, 0:ntok], sg[:, 0:ntok], ALU.mult, [pa, sg], [dst_key_fn(c)])
                yield

    def f_head(pf, ntok, h, full, sig=None, sigk=None):
        t1, t2, t3, t4 = t1r.next(), t2r.next(), t3r.next(), t4r.next()
        if sig is None:
            act(t1[:, 0:ntok], pf[:, 0:ntok], AF.Sigmoid, [pf], [t1])
            sig, sigk = t1[:, 0:ntok], [t1]
        ts("dve", t2[:, 0:ntok], sig, noml[:, h:h + 1], oml[:, h:h + 1], ALU.mult, ALU.add, sigk + LBK, [t2])
        act(t1[:, 0:ntok], sig, AF.Ln, sigk + LBK, [t1], scale=oml[:, h:h + 1], bias=lb[:, h:h + 1])
        S.op("dve", lambda e: e.tensor_tensor_scan(out=t3[:, 0:ntok], data0=smask[:, 0:ntok], data1=t1[:, 0:ntok],
                                                   initial=0.0, op0=ALU.mult, op1=ALU.add), [smask, t1], [t3])
        cl = min(64, ntok)
        t3v = t3[:, 0:ntok].rearrange("p (c j) -> p c j", j=cl)
        t4v = t4[:, 0:ntok].rearrange("p (c j) -> p c j", j=cl)
        nch = ntok // cl
        if full:
            act(t1[:, 0:ntok], t3[:, 0:ntok], AF.Exp, [t3], [t1])
            t1v = t1[:, 0:ntok].rearrange("p (c j) -> p c j", j=cl)
            cp("pool", ebl[:, h, :, :], t1v[:, :, cl - 1:cl], [t1], [(ebl, h)])
            tt("dve", qtT[:, h, :], qf[:, h, :], t1[:, 0:ntok], ALU.mult, [(qf, h), t1], [(qtT, h)])
            act(t4[:, 0:ntok], t3[:, 0:ntok], AF.Exp, [t3], [t4], scale=-1.0)
            tt("dve", ktT[:, h, :], t2[:, 0:ntok], t4[:, 0:ntok], ALU.mult, [t2, t4], [(ktT, h)])
        tt("dve", t4v, t3v, t3v[:, :, cl - 1:cl].to_broadcast([128, nch, cl]), ALU.subtract, [t3, t4], [t4])
        act(t4[:, 0:ntok], t4[:, 0:ntok], AF.Exp, [t4], [t4], scale=-1.0)
        return t2, t4

    load_norm_transpose(meta_d, NMETA, hTm, 0, hTm)
    glu_stage(hTm, NMETA, [hTm], lambda c: halo[:, c, 30 - NMETA:30], lambda c: halo)
    ptm = ptr.next()
    for uf in range(2):
        wf = wget()
        for j in range(4):
            h = 4 * uf + j
            pf = pm.next()
            mm_fm(pf[:, 0:NMETA], pf, wf, j, hTm, NMETA, [hTm])
            act(of32[:, h, 0:NMETA], pf[:, 0:NMETA], AF.Sigmoid, [pf], [("msig", h)])
    for h in range(H):
        t2, t4 = f_head(None, NMETA, h, False, sig=of32[:, h, 0:NMETA], sigk=[("msig", h)])
        tt("dve", khm[:, h, :], t2[:, 0:NMETA], t4[:, 0:NMETA], ALU.mult, [t2, t4], [(khm, h)])
        tr(ptm[0:NMETA, h, :], khm[:, h, :], ident[:], [(khm, h), ident], [ptm])
    cp("dve", khatm[:], ptm[0:NMETA, :, :], [ptm], [khatm])
    wi = [wget(), wget()]
    for half in range(2):
        pv = pm.next()
        for k in range(KC):
            mm(pv[0:NMETA, :], hTm[:, k, :], wi[half][:, k, :], k == 0, k == KC - 1, [hTm, wi[half]], [pv])
        cp("act", vm[:, half * 512:(half + 1) * 512], pv[0:NMETA, :], [pv], [vm])
    pu = [pm.next(), pm.next()]
    for h in range(H):
        mm(pu[h // 4][:, (h % 4) * 128:(h % 4 + 1) * 128], khatm[:, h, :], vm[:, h * 128:(h + 1) * 128],
           True, True, [khatm, vm], [pu[h // 4]])
    cur = 0
    for g in range(2):
        cp("dve", Sf[:, 4 * g:4 * g + 4, :], pu[g][:].rearrange("p (g t) -> p g t", g=4), [pu[g]], [(Sf, g)])
        cp("act", Sb[cur][:, 4 * g:4 * g + 4, :], Sf[:, 4 * g:4 * g + 4, :], [(Sf, g)], [(Sb[cur], g)])

    finals = []
    aTk = [(aT, c) for c in range(KC)]
    hT = hTs[0]
    for t in range(NT):
        load_norm_transpose(x_d[t * 128:(t + 1) * 128, :], 128, hT, t * 128, (hT, t))
    assert 2 * NB <= 512
    sk = lambda h: [("sigf", h), ("msig", h)] + [(of32, h // 4, p) for p in range(NT)]
    v3 = lambda ap: ap.rearrange("p (c j) -> p c j", j=64)
    mu, msq, var, rstd = (s_[:] for s_ in stat)
    cur_box = [cur]

    def C1(n):
        hT = hTs[n % 2]
        hTk = [(hT, t) for t in range(NT)]
        cp("pool", aT[:, :, 0:30], halo[:], [halo], [(aT, "h")])
        glu_stage(hT, NB, hTk, lambda c: aT[:, c, 30:30 + NB], lambda c: (aT, c))
        yield
        dgs = {}

        def build_dg(c):
            dg = dgr.next()
            dma("sp", dg[:], dscr_d[c].rearrange("p (j m) -> p j m", j=CW), reads=[("dscr", c)], writes=[dg])
            dgs[c] = dg
        build_dg(0)
        memset("dve", pst[:], 0.0, [], [pst])
        prev = None
        for c in range(KC):
            if c + 1 < KC:
                build_dg(c + 1)
            dg = dgs.pop(c)
            pc = pm.next()
            for j in range(CW):
                mm(pc[:, 0:NB], dg[:, j, :], aT[:, c, j:j + NB], j == 0, j == CW - 1, [dg, (aT, c), (aT, "h")], [pc])
            act(cfull[:, c, :], pc[:, 0:NB], AF.Identity, [pc, vecs], [(cfull, c)], bias=cb[:, c:c + 1])
            c16 = c16r.next()
            q16 = q16r.next()
            act(q16[:], pc[:, 0:NB], AF.Square, [pc, vecs], [q16], bias=cb[:, c:c + 1])
            act(c16[:], pc[:, 0:NB], AF.Identity, [pc, vecs], [c16], bias=cb[:, c:c + 1])
            if prev is not None:
                pc_, c16_, q16_ = prev
                mm(pst[:, 0:NB], ones[:], c16_[:], False, False, [ones, c16_], [pst], skip=True)
                mm(pst[:, NB:2 * NB], ones[:], q16_[:], False, False, [ones, q16_], [pst], skip=True)
            prev = (c, c16, q16)
            yield
        pc_, c16_, q16_ = prev
        mm(pst[:, 0:NB], ones[:], c16_[:], False, True, [ones, c16_], [pst], skip=True)
        mm(pst[:, NB:2 * NB], ones[:], q16_[:], False, True, [ones, q16_], [pst], skip=True)
        cp("pool", halo[:], aT[:, :, NB:NB + 30], aTk, [halo])
        S.op("act", lambda e: e.mul(out=mu, in_=pst[:, 0:NB], mul=1.0 / D), [pst], [stat[0]])
        tt("dve", msq, mu, mu, ALU.mult, [stat[0]], [stat[1]])
        stt("dve", var, pst[:, NB:2 * NB], 1.0 / D, msq, ALU.mult, ALU.subtract, [pst, stat[1]], [stat[2]])
        act(var, var, AF.Ln, [stat[2], epst], [stat[2]], bias=epst[:, 0:1])
        act(rstd, var, AF.Exp, [stat[2]], [stat[3]], scale=-0.5)
        yield

    def ln_norm(c):
        tt("dve", cfull[:, c, :], cfull[:, c, :], mu, ALU.subtract, [(cfull, c), stat[0]], [(cfull, c)])
        tt("dve", cfull[:, c, :], cfull[:, c, :], rstd, ALU.mult, [(cfull, c), stat[3]], [(cfull, c)])
        act(cfull[:, c, :], cfull[:, c, :], AF.Silu, [(cfull, c), vecs], [(cfull, c)],
            scale=lng[:, c:c + 1], bias=lnb[:, c:c + 1])

    def C2R1(n):
        t0 = n * NB
        hT = hTs[n % 2]
        hTn = hTs[(n + 1) % 2]
        hTk = [(hT, t) for t in range(NT)]
        for uz in range(2):
            wz = wget()
            for j in range(4):
                c = 4 * uz + j
                pz = pm.next()
                mm_fm(pz[:, 0:NB], pz, wz, j, hT, NB, hTk)
                sz = sgr.next()
                act(sz[:], pz[:, 0:NB], AF.Silu, [pz], [sz])
                tt("dve", gated[:, c, :], cfull[:, c, :], sz[:], ALU.mult, [(cfull, c), sz], [(gated, c)])
        gk = [(gated, c) for c in range(KC)]
        for uo in range(2):
            wm = wget()
            wo = wget()
            sms = []
            for j in range(4):
                pmc = pm.next()
                mm_fm(pmc[:, 0:NB], pmc, wm, j, hT, NB, hTk)
                sm = sgr.next()
                act(sm[:], pmc[:, 0:NB], AF.Sigmoid, [pmc], [sm])
                sms.append(sm)
            for j in range(4):
                m = 4 * uo + j
                sm = sms[j]
                py = pm.next()
                for k in range(KC):
                    mm(py[:, 0:NB], wo[:, k, j * 128:(j + 1) * 128], gated[:, k, :], k == 0, k == KC - 1, [wo] + gk, [py])
                tt("dve", ycg[:, m, :], py[:, 0:NB], sm[:], ALU.mult, [py, sm], [(ycg, m)])
        xhn = []
        if n + 1 < NBLK:
            for t in range(NT):
                xhn.append(norm_transpose_a(x_d[t0 + NB + t * 128:t0 + NB + (t + 1) * 128, :], 128))
        for uq in range(2):
            wq = wget()
            for j in range(4):
                h = 4 * uq + j
                pq = pm.next()
                mm_fm(pq[:, 0:NB], pq, wq, j, hT, NB, hTk)
                act(qf[:, h, :], pq[:, 0:NB], AF.Silu, [pq], [(qf, h)])
        for uf in range(2):
            wf = wget()
            for j in range(4):
                h = 4 * uf + j
                pf = pm.next()
                mm_fm(pf[:, 0:NB], pf, wf, j, hT, NB, hTk)
                act(of32[:, h, :], pf[:, 0:NB], AF.Sigmoid, [pf], sk(h))
        wi = [wget(), wget()]
        for t in range(NT):
            for half in range(2):
                pv = pm.next()
                for k in range(KC):
                    mm(pv[:], hT[:, k, t * 128:(t + 1) * 128], wi[half][:, k, :], k == 0, k == KC - 1,
                       [(hT, t), wi[half]], [pv])
                cp("act", vtm[:, t, half * 512:(half + 1) * 512], pv[:], [pv], [(vtm, t, half)])
        for t, xh_ in enumerate(xhn):
            norm_transpose_b(xh_, 128, hTn, t * 128, (hTn, t))

    def R2(n):
        for g0 in (0, 4):
            G = range(g0, g0 + 4)
            T = {h: (t1r.next(), t2r.next(), t3r.next(), t4r.next()) for h in G}
            for h in G:
                t1, t2, t3, t4 = T[h]
                ts("dve", t2[:], of32[:, h, :], noml[:, h:h + 1], oml[:, h:h + 1], ALU.mult, ALU.add, sk(h) + LBK, [t2])
                act(t1[:], of32[:, h, :], AF.Ln, sk(h) + LBK, [t1], scale=oml[:, h:h + 1], bias=lb[:, h:h + 1])
            yield
            for h in G:
                t1, t2, t3, t4 = T[h]
                S.op("dve", (lambda t3=t3, t1=t1: lambda e: e.tensor_tensor_scan(
                    out=t3[:], data0=smask[:], data1=t1[:], initial=0.0, op0=ALU.mult, op1=ALU.add))(), [smask, t1], [t3])
            yield
            for h in G:
                t1, t2, t3, t4 = T[h]
                act(t1[:], t3[:], AF.Exp, [t3], [t1])
                cp("pool", ebl[:, h, :, :], v3(t1[:])[:, :, 63:64], [t1], [(ebl, h)])
                tt("dve", qtT[:, h, :], qf[:, h, :], t1[:], ALU.mult, [(qf, h), t1], [(qtT, h)])
            yield
            for h in G:
                t1, t2, t3, t4 = T[h]
                act(t4[:], t3[:], AF.Exp, [t3], [t4], scale=-1.0)
                tt("dve", ktT[:, h, :], t2[:], t4[:], ALU.mult, [t2, t4], [(ktT, h)])
            yield
            for h in G:
                t1, t2, t3, t4 = T[h]
                tt("dve", v3(t4[:]), v3(t3[:]), v3(t3[:])[:, :, 63:64].to_broadcast([128, NCH, 64]), ALU.subtract,
                   [t3, t4], [t4])
                act(t4[:], t4[:], AF.Exp, [t4], [t4], scale=-1.0)
                tt("dve", khall[:, h, :], t2[:], t4[:], ALU.mult, [t2, t4], [(khall, h)])
            yield
        assert NT * 4 <= KC
        for g in range(2):
            pt = ptr.next()
            for hh in range(4):
                h = 4 * g + hh
                for t in range(NT):
                    tr(pt[:, t * 4 + hh, :], khall[:, h, t * 128:(t + 1) * 128], ident[:], [(khall, h), ident], [pt])
            cp("act", khat[:, :, 4 * g:4 * g + 4, :], pt[:, 0:NT * 4, :].rearrange("p (t h) m -> p t h m", t=NT),
               [pt], [(khat, 4 * g + hh) for hh in range(4)])
            yield
        for p in range(NT):
            vk = [(vtm, p, 0), (vtm, p, 1)]
            pa = [pm.next(), pm.next()]
            for h in range(H):
                mm(pa[h // 4][:, (h % 4) * 128:(h % 4 + 1) * 128], ktT[:, h, p * 128:(p + 1) * 128],
                   qtT[:, h, p * 128:(p + 1) * 128], True, True, [(ktT, h), (qtT, h)], [pa[h // 4]])
            at = atr.next()
            for g in range(2):
                tt("dve", at[:, 4 * g:4 * g + 4, :], pa[g][:].rearrange("p (g t) -> p g t", g=4),
                   pmask[:].unsqueeze(1).to_broadcast([128, 4, 128]), ALU.mult, [pa[g], pmask], [(at, g)])
            yield
            pus = {}

            def u_mm(ci):
                r0 = 64 * ci
                pu = [pm.next(), pm.next()]
                for h in range(H):
                    mm(pu[h // 4][:, (h % 4) * 128:(h % 4 + 1) * 128], khat[r0:r0 + 64, p, h, :],
                       vtm[r0:r0 + 64, p, h * 128:(h + 1) * 128], True, True, vk + [(khat, h)], [pu[h // 4]])
                pus[ci] = pu

            def o_mm(ci):
                cur = cur_box[0]
                r0 = 64 * ci
                for h in range(H):
                    po = pd[h // 4]
                    c0 = (h % 4) * 128 + r0
                    mm(po[:, c0:c0 + 64], vtm[r0:r0 + 64, p, h * 128:(h + 1) * 128], at[r0:r0 + 64, h, r0:r0 + 64],
                       True, False, vk + [(at, h // 4)], [po])
                    mm(po[:, c0:c0 + 64], Sb[cur][:, h, :], qtT[:, h, p * 128 + r0:p * 128 + r0 + 64],
                       False, True, [(Sb[cur], h // 4), (qtT, h)], [po])

            def s_update(ci):
                cur = cur_box[0]
                gch = 2 * p + ci
                pu = pus[ci]
                nxt = 1 - cur
                for g in range(2):
                    hs = slice(4 * g, 4 * g + 4)
                    tt("dve", Sf[:, hs, :], Sf[:, hs, :], ebl[:, hs, gch, :].to_broadcast([128, 4, 128]), ALU.mult,
                       [(Sf, g)] + [(ebl, h) for h in range(4 * g, 4 * g + 4)], [(Sf, g)])
                    tt("dve", Sf[:, hs, :], Sf[:, hs, :], pu[g][:].rearrange("p (g t) -> p g t", g=4), ALU.add,
                       [(Sf, g), pu[g]], [(Sf, g)])
                    cp("act", Sb[nxt][:, hs, :], Sf[:, hs, :], [(Sf, g)], [(Sb[nxt], g)])
                cur_box[0] = nxt

            o_mm(0)
            yield
            u_mm(0)
            s_update(0)
            yield
            o_mm(1)
            yield
            u_mm(1)
            s_update(1)
            for g in range(2):
                cp("act", of32[:, 4 * g:4 * g + 4, p * 128:(p + 1) * 128], pd[g][:].rearrange("p (g t) -> p g t", g=4),
                   [pd[g]], [(of32, g, p)] + [("sigf", 4 * g + q) for q in range(4)])
            yield

    def R3(n):
        t0 = n * NB
        hT = hTs[n % 2]
        hTk = [(hT, t) for t in range(NT)]
        xrs = []
        for t in range(NT):
            xr = xrr.next()
            dma("sp", xr[:], x_d[t0 + t * 128:t0 + (t + 1) * 128, :], writes=[xr])
            xrs.append(xr)
        def onorm_tail(h, o16):
            ofk = [(of32, h // 4, p) for p in range(NT)]
            pss = pm.next()
            mm(pss[:, 0:NB], ones[:], o16[:], True, True, [ones, o16], [pss])
            ro = ror.next()
            act(ro[:], pss[:, 0:NB], AF.Ln, [pss, epst], [ro], scale=1.0 / 128, bias=epst[:, 0:1])
            act(ro[:], ro[:], AF.Exp, [ro], [ro], scale=-0.5)
            tt("dve", of32[:, h, :], of32[:, h, :], ro[:], ALU.mult, ofk + [ro], ofk)

        pend = None
        wgs = {}
        for h in range(H):
            if h % 4 == 0:
                wgs[h // 4] = wget()
            pg = pm.next()
            mm_fm(pg[:, 0:NB], pg, wgs[h // 4], h % 4, hT, NB, hTk)
            act(graw[:, h, :], pg[:, 0:NB], AF.Identity, [pg], [(graw, h)])
            ofk = [(of32, h // 4, p) for p in range(NT)]
            o16 = q16r.next()
            act(o16[:], of32[:, h, :], AF.Square, ofk, [o16])
            if pend is not None:
                onorm_tail(*pend)
            pend = (h, o16)
        onorm_tail(*pend)
        for h in range(H):
            ofk = [(of32, h // 4, p) for p in range(NT)]
            act(graw[:, h, :], graw[:, h, :], AF.Silu, [(graw, h)], [(graw, h)])
            stt("dve", og[:, h, :], of32[:, h, :], gng[:, h:h + 1], graw[:, h, :], ALU.mult, ALU.mult,
                ofk + [vecs, (graw, h)], [(og, h)])
        ogk = [(og, h) for h in range(H)]
        for uo in range(2):
            wm = wget()
            wr = wget()
            sms = []
            for j in range(4):
                pmr = pm.next()
                mm_fm(pmr[:, 0:NB], pmr, wm, j, hT, NB, hTk)
                sm = sgr.next()
                act(sm[:], pmr[:, 0:NB], AF.Sigmoid, [pmr], [sm])
                sms.append(sm)
            for j in range(4):
                m = 4 * uo + j
                sm = sms[j]
                py = pm.next()
                for k in range(KC):
                    mm(py[:, 0:NB], wr[:, k, j * 128:(j + 1) * 128], og[:, k, :], k == 0, k == KC - 1, [wr] + ogk, [py])
                tt("dve", sm[:], py[:, 0:NB], sm[:], ALU.mult, [py, sm], [sm])
                if n + 1 < NBLK:
                    ln_norm(m)
                tt("dve", merged[:, m, :], sm[:], ycg[:, m, :], ALU.add, [sm, (ycg, m)], [(merged, m)])
        mk = [(merged, m) for m in range(KC)]
        wo2 = [wget(), wget()]
        for t in range(NT):
            xr = xrs[t]
            for half in range(2):
                pp = pm.next()
                for k in range(KC):
                    mm(pp[:], merged[:, k, t * 128:(t + 1) * 128], wo2[half][:, k, :], k == 0, k == KC - 1,
                       mk + [wo2[half]], [pp])
                tt("dve", xr[:, half * 512:(half + 1) * 512], pp[:], xr[:, half * 512:(half + 1) * 512], ALU.add,
                   [pp, xr], [xr])
            ss = small.next()
            act(junk[:], xr[:], AF.Square, [xr], [junk, (ss, 0)], accum=ss[:, 0:1])
            act(ss[:, 1:2], ss[:, 0:1], AF.Ln, [(ss, 0), epst], [(ss, 1)], scale=1.0 / D, bias=epst[:, 0:1])
            act(ss[:, 2:3], ss[:, 1:2], AF.Exp, [(ss, 1)], [(ss, 2)], scale=-0.5)
            stt("dve", xr[:], xr[:], ss[:, 2:3], fgbc[:], ALU.mult, ALU.mult, [xr, (ss, 2), fgbc], [xr])
            finals.append(dma("sp", out_d[t0 + t * 128:t0 + (t + 1) * 128, :], xr[:], reads=[xr]))

    def interleave(ga, na, gb, nb):
        ia = ib = 0
        da = db = False
        while not (da and db):
            if not da and (db or ia * nb <= ib * na):
                try:
                    next(ga)
                    ia += 1
                except StopIteration:
                    da = True
            else:
                try:
                    next(gb)
                    ib += 1
                except StopIteration:
                    db = True

    def interleave_at(ga, gb, pos):
        for i, _ in enumerate(ga):
            if i in pos:
                next(gb, None)
        for _ in gb:
            pass

    for _ in C1(0):
        pass
    for c_ in range(KC):
        ln_norm(c_)
    for n in range(NBLK):
        C2R1(n)
        if n + 1 < NBLK:
            gc = C1(n + 1)
            next(gc)
            interleave_at(R2(n), gc, (0, 1, 2, 3, 4, 5, 6, 7))
        else:
            for _ in R2(n):
                pass
        R3(n)
    counts = S.emit(final_waits=finals)
    return nc, counts


SEQ_FULL = 4096
NB_FULL = 256
_CACHE = {}


def kernel(**inputs):
    x = np.asarray(inputs["x"], np.float32)
    ncore = x.shape[0]
    lay = host_layout(inputs)
    if "nc" not in _CACHE:
        _CACHE["nc"] = build_program(SEQ_FULL, NB_FULL)[0]
    nc = _CACHE["nc"]
    in_maps = []
    for c in range(ncore):
        im = dict(lay)
        im["x"] = np.ascontiguousarray(x[c])
        in_maps.append(im)
    res = run_bass_kernel_spmd(nc, in_maps, core_ids=list(range(ncore)))
    return np.stack([np.asarray(r["out"], np.float32) for r in res.results], axis=0)
```

```python
import numpy as np
import concourse.bass as bass
import concourse.mybir as mybir
from concourse.bass_utils import run_bass_kernel_spmd

F32 = mybir.dt.float32
BF16 = mybir.dt.bfloat16
AF = mybir.ActivationFunctionType
ALU = mybir.AluOpType

ENGS = ("pe", "act", "dve", "pool", "sp")


class Buf:
    _n = 0

    def __init__(self, t, name=None):
        self.t = t
        Buf._n += 1
        self.key = (name or "buf", Buf._n)

    def __getitem__(self, k):
        return self.t[k]


class Sched:
    def __init__(self, nc, n_dma_sems=8):
        self.nc = nc
        self.eng = dict(pe=nc.tensor, act=nc.scalar, dve=nc.vector, pool=nc.gpsimd, sp=nc.sync)
        self.ops = {e: [] for e in ENGS}
        self.last_w = {}
        self.readers = {}
        self.n_dma_sems = n_dma_sems
        self.dma_count = {e: 0 for e in ENGS}
        self.dma_sem_uses = {}
        self.dma_last = {}

    @staticmethod
    def _k(b):
        if isinstance(b, Buf):
            return b.key
        if isinstance(b, tuple) and len(b) > 0 and isinstance(b[0], Buf):
            return (b[0].key,) + tuple(b[1:])
        return b

    def op(self, e, fn, reads=(), writes=(), dma=False):
        idx = len(self.ops[e])
        deps = []
        rk = [self._k(b) for b in reads]
        wk = [self._k(b) for b in writes]
        for k in rk:
            lw = self.last_w.get(k)
            if lw is not None:
                deps.append(lw)
        for k in wk:
            lw = self.last_w.get(k)
            if lw is not None:
                deps.append(lw)
            deps.extend(self.readers.get(k, ()))
        rec = dict(eng=e, idx=idx, fn=fn, deps=deps, dma=dma, signal=False)
        if dma:
            slot = self.dma_count[e] % self.n_dma_sems
            self.dma_count[e] += 1
            n = self.dma_sem_uses.get((e, slot), 0) + 1
            self.dma_sem_uses[(e, slot)] = n
            rec["dslot"] = (e, slot)
            rec["dval"] = 16 * n
            prev = self.dma_last.get((e, slot))
            if prev is not None:
                deps.append(prev)
            tok = ("dma", e, slot, 16 * n)
            self.dma_last[(e, slot)] = tok
        else:
            tok = ("eng", e, idx)
        rec["tok"] = tok
        self.ops[e].append(rec)
        for k in rk:
            self.readers.setdefault(k, []).append(tok)
        for k in wk:
            self.last_w[k] = tok
            self.readers[k] = []
        return tok

    def emit(self, final_waits=()):
        nc = self.nc
        need = {e: [] for e in ENGS}
        for e in ENGS:
            seen = {}
            for rec in self.ops[e]:
                w = {}
                for d in rec["deps"]:
                    if d[0] == "eng":
                        _, pe_, pidx = d
                        if pe_ == e:
                            if e in ("pe", "sp"):
                                continue
                            if pidx >= rec["idx"]:
                                continue
                        key = ("eng", pe_)
                        val = pidx
                    else:
                        _, qe, slot, v = d
                        key = ("dma", qe, slot)
                        val = v
                    if seen.get(key, -1) >= val:
                        continue
                    if w.get(key, -1) < val:
                        w[key] = val
                for key, val in w.items():
                    seen[key] = val
                    if key[0] == "eng":
                        self.ops[key[1]][val]["signal"] = True
                rec["waits"] = w
        for e in ENGS:
            c = 0
            for rec in self.ops[e]:
                if rec["signal"]:
                    c += 1
                    rec["sigval"] = c
        esem = {e: nc.alloc_semaphore("s_" + e) for e in ENGS}
        dsem = {}
        for (e, slot) in self.dma_sem_uses:
            dsem[(e, slot)] = nc.alloc_semaphore("d_%s%d" % (e, slot))
        for e in ENGS:
            eh = self.eng[e]
            for rec in self.ops[e]:
                for key, val in rec["waits"].items():
                    if key[0] == "eng":
                        sv = self.ops[key[1]][val]["sigval"]
                        eh.wait_ge(esem[key[1]], sv)
                    else:
                        eh.wait_ge(dsem[(key[1], key[2])], val)
                ins = rec["fn"](eh)
                if rec["dma"]:
                    ins.then_inc(dsem[rec["dslot"]], 16)
                elif rec["signal"]:
                    ins.then_inc(esem[e], 1)
        eh = self.eng["sp"]
        for tok in final_waits:
            _, qe, slot, v = tok
            eh.wait_ge(dsem[(qe, slot)], v)
        n = {e: len(self.ops[e]) for e in ENGS}
        return n


D = 1024
KC = 8
H = 8
NMETA = 16
CW = 31
EPS = 1e-6
NUNIT = 24

_SEG = dict(glu_a=0, glu_b=1024, z=2048, q=3072, f=4096, i=5120, g=6144, mc=7168, mr=8192,
            wco=9216, wro=10240, wo=11264)
_UNIT_COLS = [
    ("glu_b", 0), ("glu_a", 0), ("glu_b", 1), ("glu_a", 1),
    ("z", 0), ("z", 1),
    ("mc", 0), ("wco", 0), ("mc", 1), ("wco", 1),
    ("q", 0), ("q", 1), ("f", 0), ("f", 1),
    ("i", 0), ("i", 1), ("g", 0), ("g", 1),
    ("mr", 0), ("wro", 0), ("mr", 1), ("wro", 1),
    ("wo", 0), ("wo", 1),
]


def host_layout(inputs):
    f32 = np.float32
    w_in = np.asarray(inputs["w_in"], f32)[0]
    wcat = np.concatenate([w_in, np.asarray(inputs["w_conv_out"], f32)[0],
                           np.asarray(inputs["w_rec_out"], f32)[0], np.asarray(inputs["w_out"], f32)[0]], axis=1)
    wall = np.empty((NUNIT, 128, KC, 512), f32)
    for u, (seg, half) in enumerate(_UNIT_COLS):
        c0 = _SEG[seg] + 512 * half
        blk = wcat[:, c0:c0 + 512]
        wall[u] = blk.reshape(KC, 128, 512).transpose(1, 0, 2)
    wall = wall.reshape(NUNIT, 128, KC * 512)
    fm = lambda v: np.ascontiguousarray(np.asarray(v, f32).reshape(KC, 128).T)
    lbl = np.asarray(inputs["lb_logits"], f32)
    vecs = np.concatenate([fm(inputs["norm_g"][0]), fm(inputs["conv_b"][0]), fm(inputs["ln_g"][0]),
                           fm(inputs["ln_b"][0]), fm(inputs["gnorm_g"][0]), fm(lbl[0]), fm(lbl[1])], axis=1)
    cw = np.asarray(inputs["conv_w"], f32)[0]
    convw = np.ascontiguousarray(cw.reshape(CW, KC, 128).transpose(2, 1, 0)).reshape(128, KC * CW)
    gbc = np.ascontiguousarray(np.broadcast_to(np.asarray(inputs["norm_g"], f32)[0][None, :], (128, D)))
    fgbc = np.ascontiguousarray(np.broadcast_to(np.asarray(inputs["final_g"], f32)[None, :], (128, D)))
    return dict(wall=wall, vecs=np.ascontiguousarray(vecs), convw=convw, gbc=gbc, fgbc=fgbc,
                meta=np.ascontiguousarray(np.asarray(inputs["meta_tokens"], f32)))


class Ring:
    def __init__(self, mk, name, shape, dt, n):
        self.bufs = [mk("%s%d" % (name, i), shape, dt) for i in range(n)]
        self.i = 0

    def next(self):
        b = self.bufs[self.i % len(self.bufs)]
        self.i += 1
        return b


def build_program(SEQ, NB, n_dma_sems=16):
    nc = bass.Bass("TRN2", target_bir_lowering=False)
    S = Sched(nc, n_dma_sems=n_dma_sems)
    NT = NB // 128
    NCH = NB // 64
    NBLK = SEQ // NB
    assert NB % 128 == 0 and SEQ % NB == 0

    def dram(name, shape, dt, kind):
        return nc.dram_tensor(name, list(shape), dt, kind=kind).ap()
    x_d = dram("x", [SEQ, D], F32, "ExternalInput")
    meta_d = dram("meta", [NMETA, D], F32, "ExternalInput")
    wall_d = dram("wall", [NUNIT, 128, KC * 512], F32, "ExternalInput")
    vecs_d = dram("vecs", [128, 56], F32, "ExternalInput")
    convw_d = dram("convw", [128, KC * CW], F32, "ExternalInput")
    gbc_d = dram("gbc", [128, D], F32, "ExternalInput")
    fgbc_d = dram("fgbc", [128, D], F32, "ExternalInput")
    out_d = dram("out", [SEQ, D], F32, "ExternalOutput")
    wscr_d = nc.dram_tensor("wscr", [NUNIT, 128, KC * 512], BF16, kind="Internal").ap()
    dscr_d = nc.dram_tensor("dscr", [KC, 128, CW * 128], BF16, kind="Internal").ap()

    def sb(name, shape, dt):
        return Buf(nc.alloc_sbuf_tensor("s_" + name, list(shape), dt), name)

    def psb(name, shape, dt=F32):
        return Buf(nc.alloc_psum_tensor("p_" + name, list(shape), dt), name)

    def dma(eng, out, in_, reads=(), writes=()):
        return S.op(eng, lambda e: e.dma_start(out=out, in_=in_), reads, writes, dma=True)

    def act(out, in_, func, reads, writes, scale=1.0, bias=None, accum=None):
        kw = dict(out=out, in_=in_, func=func, scale=scale)
        if bias is not None:
            kw["bias"] = bias
        if accum is not None:
            kw["accum_out"] = accum
        return S.op("act", lambda e: e.activation(**kw), reads, writes)

    def tt(eng, out, in0, in1, op, reads, writes):
        return S.op(eng, lambda e: e.tensor_tensor(out=out, in0=in0, in1=in1, op=op), reads, writes)

    def ts(eng, out, in0, s1, s2, op0, op1, reads, writes):
        if s2 is None:
            return S.op(eng, lambda e: e.tensor_scalar(out=out, in0=in0, scalar1=s1, scalar2=None, op0=op0), reads, writes)
        return S.op(eng, lambda e: e.tensor_scalar(out=out, in0=in0, scalar1=s1, scalar2=s2, op0=op0, op1=op1), reads, writes)

    def stt(eng, out, in0, scalar, in1, op0, op1, reads, writes):
        return S.op(eng, lambda e: e.scalar_tensor_tensor(out=out, in0=in0, scalar=scalar, in1=in1, op0=op0, op1=op1),
                    reads, writes)

    def cp(eng, out, in_, reads, writes):
        if eng == "act":
            return S.op("act", lambda e: e.copy(out=out, in_=in_), reads, writes)
        return S.op(eng, lambda e: e.tensor_copy(out=out, in_=in_), reads, writes)

    def mm(out, lhsT, rhs, start, stop, reads, writes, skip=False):
        if skip:
            return S.op("pe", lambda e: e.matmul(out, lhsT=lhsT, rhs=rhs, start=start, stop=stop,
                                                 skip_group_check=True), reads, writes)
        return S.op("pe", lambda e: e.matmul(out, lhsT=lhsT, rhs=rhs, start=start, stop=stop), reads, writes)

    def tr(out, in_, ident_ap, reads, writes):
        return S.op("pe", lambda e: e.transpose(out=out, in_=in_, identity=ident_ap), reads, writes)

    def memset(eng, ap, val, reads, writes):
        return S.op(eng, lambda e: e.memset(ap, val), reads, writes)

    ident32 = sb("ident32", [128, 128], F32)
    ident = sb("ident", [128, 128], BF16)
    ones = sb("ones", [128, 128], BF16)
    pmask = sb("pmask", [128, 128], F32)
    smask = sb("smask", [128, NB], F32)
    epst = sb("epst", [128, 1], F32)
    vecs = sb("vecs", [128, 56], F32)
    lbt = sb("lbt", [128, 32], F32)
    convw = sb("convw", [128, KC, CW], F32)
    convw16 = sb("convw16", [128, KC, CW], BF16)
    gbc = sb("gbc", [128, D], F32)
    fgbc = sb("fgbc", [128, D], F32)
    wring = Ring(sb, "wb", [128, KC, 512], BF16, 5)
    xin = Ring(sb, "xin", [128, D], F32, 2)
    junk = sb("junk", [128, D], BF16)
    small = Ring(sb, "small", [128, 4], F32, 4)
    xhr = Ring(sb, "xh", [128, D], BF16, 2)
    hTs = [sb("hT%d" % i, [128, KC, NB], BF16) for i in range(2)]
    hTm = sb("hTm", [128, KC, NMETA], BF16)
    aT = sb("aT", [128, KC, 30 + NB], BF16)
    halo = sb("halo", [128, KC, 30], BF16)
    sgr = Ring(sb, "sg", [128, NB], F32, 4)
    dgr = Ring(sb, "dg", [128, CW, 128], BF16, 2)
    cfull = sb("cfull", [128, KC, NB], F32)
    c16r = Ring(sb, "c16", [128, NB], BF16, 2)
    q16r = Ring(sb, "q16", [128, NB], BF16, 2)
    stat = [sb("stat%d" % i, [128, NB], F32) for i in range(4)]
    gated = sb("gated", [128, KC, NB], BF16)
    ycg = sb("ycg", [128, KC, NB], BF16)
    qf = sb("qf", [128, H, NB], F32)
    t1r = Ring(sb, "t1", [128, NB], F32, 4)
    t2r = Ring(sb, "t2", [128, NB], F32, 4)
    t3r = Ring(sb, "t3", [128, NB], F32, 4)
    t4r = Ring(sb, "t4", [128, NB], F32, 4)
    khall = sb("khall", [128, H, NB], BF16)
    ebl = sb("ebl", [128, H, NCH, 1], F32)
    qtT = sb("qtT", [128, H, NB], BF16)
    ktT = sb("ktT", [128, H, NB], BF16)
    khat = sb("khat", [128, NT, H, 128], BF16)
    vtm = sb("vtm", [128, NT, D], BF16)
    khm = sb("khm", [128, H, NMETA], BF16)
    khatm = sb("khatm", [NMETA, H, 128], BF16)
    vm = sb("vm", [NMETA, D], BF16)
    atr = Ring(sb, "at", [128, H, 128], BF16, 2)
    Sf = sb("Sf", [128, H, 128], F32)
    Sb = [sb("Sb%d" % i, [128, H, 128], BF16) for i in range(2)]
    of32 = sb("of32", [128, H, NB], F32)
    graw = sb("graw", [128, H, NB], F32)
    ror = Ring(sb, "ro", [128, NB], F32, 2)
    og = sb("og", [128, KC, NB], BF16)
    merged = sb("merged", [128, KC, NB], BF16)
    xrr = xin
    pm = Ring(psb, "pm", [128, 512], F32, 4)
    pd = [psb("pd%d" % i, [128, 512], F32) for i in range(2)]
    pst = psb("pst", [128, 512], F32)
    ptr = Ring(psb, "pt", [128, KC, 128], BF16, 1)

    ng, cb, lng, lnb, gng = (vecs[:, 8 * i:8 * i + 8] for i in range(5))
    lb, oml, noml = lbt[:, 8:16], lbt[:, 16:24], lbt[:, 24:32]

    dma("sp", vecs[:], vecs_d, writes=[vecs])
    dma("sp", convw[:], convw_d.rearrange("p (c j) -> p c j", c=KC), writes=[convw])
    dma("sp", gbc[:], gbc_d, writes=[gbc])
    dma("sp", fgbc[:], fgbc_d, writes=[fgbc])
    memset("pool", ident32[:], 0.0, [], [ident32])
    S.op("pool", lambda e: e.affine_select(out=ident32[:], in_=ident32[:], pattern=[[-1, 128]], compare_op=ALU.not_equal,
                                            fill=1.0, base=0, channel_multiplier=1), [ident32], [ident32])
    cp("pool", ident[:], ident32[:], [ident32], [ident])
    memset("pool", ones[:], 1.0, [], [ones])
    memset("pool", pmask[:], 1.0, [], [pmask])
    S.op("pool", lambda e: e.affine_select(out=pmask[:], in_=pmask[:], pattern=[[1, 128]], compare_op=ALU.is_ge,
                                            fill=0.0, base=0, channel_multiplier=-1), [pmask], [pmask])
    memset("pool", pmask[0:64, 64:128], 0.0, [pmask], [pmask])
    memset("pool", smask[:], 1.0, [], [smask])
    memset("pool", smask[:].rearrange("p (c j) -> p c j", j=64)[:, :, 0:1], 0.0, [smask], [smask])
    memset("pool", epst[:], EPS, [], [epst])
    memset("pool", halo[:], 0.0, [], [halo])
    tt("dve", lbt[:, 0:8], vecs[:, 40:48], vecs[:, 48:56], ALU.subtract, [vecs], [(lbt, 0)])
    act(lbt[:, 8:16], lbt[:, 0:8], AF.Sigmoid, [(lbt, 0)], [(lbt, 1)])
    act(lbt[:, 16:24], lbt[:, 0:8], AF.Sigmoid, [(lbt, 0)], [(lbt, 2)], scale=-1.0)
    ts("dve", lbt[:, 24:32], lbt[:, 16:24], -1.0, None, ALU.mult, None, [(lbt, 2)], [(lbt, 3)])
    LBK = [(lbt, 1), (lbt, 2), (lbt, 3)]
    cp("dve", convw16[:], convw[:], [convw], [convw16])

    for c_ in range(KC):
        dg_ = dgr.next()
        tt("dve", dg_[:], ident[:].unsqueeze(1).to_broadcast([128, CW, 128]),
           convw16[:, c_, :].unsqueeze(2).to_broadcast([128, CW, 128]), ALU.mult, [ident, convw16], [dg_])
        dma("sp", dscr_d[c_].rearrange("p (j m) -> p j m", j=CW), dg_[:], reads=[dg_], writes=[("dscr", c_)])

    seq = [0, 1, 2, 3, 12, 13, 14, 15] + [0, 1, 2, 3]
    for n_ in range(NBLK):
        seq += list(range(4, 16)) + ([0, 1, 2, 3] if n_ + 1 < NBLK else []) + list(range(16, NUNIT))
    converted = set()
    cast_i = [0]
    state = dict(issued=0, taken=0, bufs={})

    def convert(u):
        if u in converted:
            return
        converted.add(u)
        dma("pool", wscr_d[u], wall_d[u], writes=[("wscr", u)])
    for u_ in seq[:8 + NUNIT]:
        convert(u_)

    def issue_upto(i):
        while state["issued"] <= i and state["issued"] < len(seq):
            u = seq[state["issued"]]
            convert(u)
            b = wring.next()
            dma("sp", b[:], wscr_d[u].rearrange("p (k n) -> p k n", k=KC),
                reads=[("wscr", u)], writes=[b])
            state["bufs"][state["issued"]] = b
            state["issued"] += 1

    def wget():
        i = state["taken"]
        issue_upto(i + 3)
        state["taken"] += 1
        return state["bufs"].pop(i)

    def load_norm_transpose(src, ntok, dst, col0, key):
        norm_transpose_b(norm_transpose_a(src, ntok), ntok, dst, col0, key)

    def norm_transpose_a(src, ntok):
        xt = xin.next()
        dma("sp", xt[0:ntok, :], src, writes=[xt])
        ss = small.next()
        act(junk[0:ntok, :], xt[0:ntok, :], AF.Square, [xt], [junk, (ss, 0)], accum=ss[0:ntok, 0:1])
        act(ss[0:ntok, 1:2], ss[0:ntok, 0:1], AF.Ln, [(ss, 0), epst], [(ss, 1)], scale=1.0 / D, bias=epst[0:ntok, 0:1])
        act(ss[0:ntok, 2:3], ss[0:ntok, 1:2], AF.Exp, [(ss, 1)], [(ss, 2)], scale=-0.5)
        xh = xhr.next()
        stt("dve", xh[0:ntok, :], xt[0:ntok, :], ss[0:ntok, 2:3], gbc[0:ntok, :], ALU.mult, ALU.mult,
            [xt, (ss, 2), gbc], [xh])
        return xh

    def norm_transpose_b(xh, ntok, dst, col0, key):
        pt = ptr.next()
        for k in range(KC):
            tr(pt[:, k, 0:ntok], xh[0:ntok, k * 128:(k + 1) * 128], ident[0:ntok, 0:ntok], [xh, ident], [pt])
        cp("dve", dst[:, :, col0:col0 + ntok], pt[:, :, 0:ntok], [pt], [key])

    def mm_fm(ps_ap, psbuf, w, j, actT, ntok, akeys):
        for k in range(KC):
            mm(ps_ap, w[:, k, j * 128:(j + 1) * 128], actT[:, k, 0:ntok], k == 0, k == KC - 1, [w] + akeys, [psbuf])

    def glu_stage(actT, ntok, akeys, dst_fn, dst_key_fn):
        for _ in glu_steps(actT, ntok, akeys, dst_fn, dst_key_fn):
            pass

    def glu_steps(actT, ntok, akeys, dst_fn, dst_key_fn):
        for uc in range(2):
            wb = wget()
            wa = wget()
            for j in range(4):
                c = 4 * uc + j
                pb = pm.next()
                mm_fm(pb[:, 0:ntok], pb, wb, j, actT, ntok, akeys)
                sg = sgr.next()
                act(sg[:, 0:ntok], pb[:, 0:ntok], AF.Sigmoid, [pb], [sg])
                pa = pm.next()
                mm_fm(pa[:, 0:ntok], pa, wa, j, actT, ntok, akeys)
                tt("dve", dst_fn(c), pa[:, 0:ntok], sg[:, 0:ntok], ALU.mult, [pa, sg], [dst_key_fn(c)])
                yield

    def f_head(pf, ntok, h, full, sig=None, sigk=None):
        t1, t2, t3, t4 = t1r.next(), t2r.next(), t3r.next(), t4r.next()
        if sig is None:
            act(t1[:, 0:ntok], pf[:, 0:ntok], AF.Sigmoid, [pf], [t1])
            sig, sigk = t1[:, 0:ntok], [t1]
        ts("dve", t2[:, 0:ntok], sig, noml[:, h:h + 1], oml[:, h:h + 1], ALU.mult, ALU.add, sigk + LBK, [t2])
        act(t1[:, 0:ntok], sig, AF.Ln, sigk + LBK, [t1], scale=oml[:, h:h + 1], bias=lb[:, h:h + 1])
        S.op("dve", lambda e: e.tensor_tensor_scan(out=t3[:, 0:ntok], data0=smask[:, 0:ntok], data1=t1[:, 0:ntok],
                                                   initial=0.0, op0=ALU.mult, op1=ALU.add), [smask, t1], [t3])
        cl = min(64, ntok)
        t3v = t3[:, 0:ntok].rearrange("p (c j) -> p c j", j=cl)
        t4v = t4[:, 0:ntok].rearrange("p (c j) -> p c j", j=cl)
        nch = ntok // cl
        if full:
            act(t1[:, 0:ntok], t3[:, 0:ntok], AF.Exp, [t3], [t1])
            t1v = t1[:, 0:ntok].rearrange("p (c j) -> p c j", j=cl)
            cp("pool", ebl[:, h, :, :], t1v[:, :, cl - 1:cl], [t1], [(ebl, h)])
            tt("dve", qtT[:, h, :], qf[:, h, :], t1[:, 0:ntok], ALU.mult, [(qf, h), t1], [(qtT, h)])
            act(t4[:, 0:ntok], t3[:, 0:ntok], AF.Exp, [t3], [t4], scale=-1.0)
            tt("dve", ktT[:, h, :], t2[:, 0:ntok], t4[:, 0:ntok], ALU.mult, [t2, t4], [(ktT, h)])
        tt("dve", t4v, t3v, t3v[:, :, cl - 1:cl].to_broadcast([128, nch, cl]), ALU.subtract, [t3, t4], [t4])
        act(t4[:, 0:ntok], t4[:, 0:ntok], AF.Exp, [t4], [t4], scale=-1.0)
        return t2, t4

    load_norm_transpose(meta_d, NMETA, hTm, 0, hTm)
    glu_stage(hTm, NMETA, [hTm], lambda c: halo[:, c, 30 - NMETA:30], lambda c: halo)
    ptm = ptr.next()
    for uf in range(2):
        wf = wget()
        for j in range(4):
            h = 4 * uf + j
            pf = pm.next()
            mm_fm(pf[:, 0:NMETA], pf, wf, j, hTm, NMETA, [hTm])
            act(of32[:, h, 0:NMETA], pf[:, 0:NMETA], AF.Sigmoid, [pf], [("msig", h)])
    for h in range(H):
        t2, t4 = f_head(None, NMETA, h, False, sig=of32[:, h, 0:NMETA], sigk=[("msig", h)])
        tt("dve", khm[:, h, :], t2[:, 0:NMETA], t4[:, 0:NMETA], ALU.mult, [t2, t4], [(khm, h)])
        tr(ptm[0:NMETA, h, :], khm[:, h, :], ident[:], [(khm, h), ident], [ptm])
    cp("dve", khatm[:], ptm[0:NMETA, :, :], [ptm], [khatm])
    wi = [wget(), wget()]
    for half in range(2):
        pv = pm.next()
        for k in range(KC):
            mm(pv[0:NMETA, :], hTm[:, k, :], wi[half][:, k, :], k == 0, k == KC - 1, [hTm, wi[half]], [pv])
        cp("act", vm[:, half * 512:(half + 1) * 512], pv[0:NMETA, :], [pv], [vm])
    pu = [pm.next(), pm.next()]
    for h in range(H):
        mm(pu[h // 4][:, (h % 4) * 128:(h % 4 + 1) * 128], khatm[:, h, :], vm[:, h * 128:(h + 1) * 128],
           True, True, [khatm, vm], [pu[h // 4]])
    cur = 0
    for g in range(2):
        cp("dve", Sf[:, 4 * g:4 * g + 4, :], pu[g][:].rearrange("p (g t) -> p g t", g=4), [pu[g]], [(Sf, g)])
        cp("act", Sb[cur][:, 4 * g:4 * g + 4, :], Sf[:, 4 * g:4 * g + 4, :], [(Sf, g)], [(Sb[cur], g)])

    finals = []
    aTk = [(aT, c) for c in range(KC)]
    hT = hTs[0]
    for t in range(NT):
        load_norm_transpose(x_d[t * 128:(t + 1) * 128, :], 128, hT, t * 128, (hT, t))
    assert 2 * NB <= 512
    sk = lambda h: [("sigf", h), ("msig", h)] + [(of32, h // 4, p) for p in range(NT)]
    v3 = lambda ap: ap.rearrange("p (c j) -> p c j", j=64)
    mu, msq, var, rstd = (s_[:] for s_ in stat)
    cur_box = [cur]

    def C1(n):
        hT = hTs[n % 2]
        hTk = [(hT, t) for t in range(NT)]
        cp("pool", aT[:, :, 0:30], halo[:], [halo], [(aT, "h")])
        glu_stage(hT, NB, hTk, lambda c: aT[:, c, 30:30 + NB], lambda c: (aT, c))
        yield
        dgs = {}

        def build_dg(c):
            dg = dgr.next()
            dma("sp", dg[:], dscr_d[c].rearrange("p (j m) -> p j m", j=CW), reads=[("dscr", c)], writes=[dg])
            dgs[c] = dg
        build_dg(0)
        memset("dve", pst[:], 0.0, [], [pst])
        prev = None
        for c in range(KC):
            if c + 1 < KC:
                build_dg(c + 1)
            dg = dgs.pop(c)
            pc = pm.next()
            for j in range(CW):
                mm(pc[:, 0:NB], dg[:, j, :], aT[:, c, j:j + NB], j == 0, j == CW - 1, [dg, (aT, c), (aT, "h")], [pc])
            act(cfull[:, c, :], pc[:, 0:NB], AF.Identity, [pc, vecs], [(cfull, c)], bias=cb[:, c:c + 1])
            c16 = c16r.next()
            q16 = q16r.next()
            act(q16[:], pc[:, 0:NB], AF.Square, [pc, vecs], [q16], bias=cb[:, c:c + 1])
            act(c16[:], pc[:, 0:NB], AF.Identity, [pc, vecs], [c16], bias=cb[:, c:c + 1])
            if prev is not None:
                pc_, c16_, q16_ = prev
                mm(pst[:, 0:NB], ones[:], c16_[:], False, False, [ones, c16_], [pst], skip=True)
                mm(pst[:, NB:2 * NB], ones[:], q16_[:], False, False, [ones, q16_], [pst], skip=True)
            prev = (c, c16, q16)
            yield
        pc_, c16_, q16_ = prev
        mm(pst[:, 0:NB], ones[:], c16_[:], False, True, [ones, c16_], [pst], skip=True)
        mm(pst[:, NB:2 * NB], ones[:], q16_[:], False, True, [ones, q16_], [pst], skip=True)
        cp("pool", halo[:], aT[:, :, NB:NB + 30], aTk, [halo])
        S.op("act", lambda e: e.mul(out=mu, in_=pst[:, 0:NB], mul=1.0 / D), [pst], [stat[0]])
        tt("dve", msq, mu, mu, ALU.mult, [stat[0]], [stat[1]])
        stt("dve", var, pst[:, NB:2 * NB], 1.0 / D, msq, ALU.mult, ALU.subtract, [pst, stat[1]], [stat[2]])
        act(var, var, AF.Ln, [stat[2], epst], [stat[2]], bias=epst[:, 0:1])
        act(rstd, var, AF.Exp, [stat[2]], [stat[3]], scale=-0.5)
        yield

    def ln_norm(c):
        tt("dve", cfull[:, c, :], cfull[:, c, :], mu, ALU.subtract, [(cfull, c), stat[0]], [(cfull, c)])
        tt("dve", cfull[:, c, :], cfull[:, c, :], rstd, ALU.mult, [(cfull, c), stat[3]], [(cfull, c)])
        act(cfull[:, c, :], cfull[:, c, :], AF.Silu, [(cfull, c), vecs], [(cfull, c)],
            scale=lng[:, c:c + 1], bias=lnb[:, c:c + 1])

    def C2R1(n):
        t0 = n * NB
        hT = hTs[n % 2]
        hTn = hTs[(n + 1) % 2]
        hTk = [(hT, t) for t in range(NT)]
        for uz in range(2):
            wz = wget()
            for j in range(4):
                c = 4 * uz + j
                pz = pm.next()
                mm_fm(pz[:, 0:NB], pz, wz, j, hT, NB, hTk)
                sz = sgr.next()
                act(sz[:], pz[:, 0:NB], AF.Silu, [pz], [sz])
                tt("dve", gated[:, c, :], cfull[:, c, :], sz[:], ALU.mult, [(cfull, c), sz], [(gated, c)])
        gk = [(gated, c) for c in range(KC)]
        for uo in range(2):
            wm = wget()
            wo = wget()
            sms = []
            for j in range(4):
                pmc = pm.next()
                mm_fm(pmc[:, 0:NB], pmc, wm, j, hT, NB, hTk)
                sm = sgr.next()
                act(sm[:], pmc[:, 0:NB], AF.Sigmoid, [pmc], [sm])
                sms.append(sm)
            for j in range(4):
                m = 4 * uo + j
                sm = sms[j]
                py = pm.next()
                for k in range(KC):
                    mm(py[:, 0:NB], wo[:, k, j * 128:(j + 1) * 128], gated[:, k, :], k == 0, k == KC - 1, [wo] + gk, [py])
                tt("dve", ycg[:, m, :], py[:, 0:NB], sm[:], ALU.mult, [py, sm], [(ycg, m)])
        xhn = []
        if n + 1 < NBLK:
            for t in range(NT):
                xhn.append(norm_transpose_a(x_d[t0 + NB + t * 128:t0 + NB + (t + 1) * 128, :], 128))
        for uq in range(2):
            wq = wget()
            for j in range(4):
                h = 4 * uq + j
                pq = pm.next()
                mm_fm(pq[:, 0:NB], pq, wq, j, hT, NB, hTk)
                act(qf[:, h, :], pq[:, 0:NB], AF.Silu, [pq], [(qf, h)])
        for uf in range(2):
            wf = wget()
            for j in range(4):
                h = 4 * uf + j
                pf = pm.next()
                mm_fm(pf[:, 0:NB], pf, wf, j, hT, NB, hTk)
                act(of32[:, h, :], pf[:, 0:NB], AF.Sigmoid, [pf], sk(h))
        wi = [wget(), wget()]
        for t in range(NT):
            for half in range(2):
                pv = pm.next()
                for k in range(KC):
                    mm(pv[:], hT[:, k, t * 128:(t + 1) * 128], wi[half][:, k, :], k == 0, k == KC - 1,
                       [(hT, t), wi[half]], [pv])
                cp("act", vtm[:, t, half * 512:(half + 1) * 512], pv[:], [pv], [(vtm, t, half)])
        for t, xh_ in enumerate(xhn):
            norm_transpose_b(xh_, 128, hTn, t * 128, (hTn, t))

    def R2(n):
        for g0 in (0, 4):
            G = range(g0, g0 + 4)
            T = {h: (t1r.next(), t2r.next(), t3r.next(), t4r.next()) for h in G}
            for h in G:
                t1, t2, t3, t4 = T[h]
                ts("dve", t2[:], of32[:, h, :], noml[:, h:h + 1], oml[:, h:h + 1], ALU.mult, ALU.add, sk(h) + LBK, [t2])
                act(t1[:], of32[:, h, :], AF.Ln, sk(h) + LBK, [t1], scale=oml[:, h:h + 1], bias=lb[:, h:h + 1])
            yield
            for h in G:
                t1, t2, t3, t4 = T[h]
                S.op("dve", (lambda t3=t3, t1=t1: lambda e: e.tensor_tensor_scan(
                    out=t3[:], data0=smask[:], data1=t1[:], initial=0.0, op0=ALU.mult, op1=ALU.add))(), [smask, t1], [t3])
            yield
            for h in G:
                t1, t2, t3, t4 = T[h]
                act(t1[:], t3[:], AF.Exp, [t3], [t1])
                cp("pool", ebl[:, h, :, :], v3(t1[:])[:, :, 63:64], [t1], [(ebl, h)])
                tt("dve", qtT[:, h, :], qf[:, h, :], t1[:], ALU.mult, [(qf, h), t1], [(qtT, h)])
            yield
            for h in G:
                t1, t2, t3, t4 = T[h]
                act(t4[:], t3[:], AF.Exp, [t3], [t4], scale=-1.0)
                tt("dve", ktT[:, h, :], t2[:], t4[:], ALU.mult, [t2, t4], [(ktT, h)])
            yield
            for h in G:
                t1, t2, t3, t4 = T[h]
                tt("dve", v3(t4[:]), v3(t3[:]), v3(t3[:])[:, :, 63:64].to_broadcast([128, NCH, 64]), ALU.subtract,
                   [t3, t4], [t4])
                act(t4[:], t4[:], AF.Exp, [t4], [t4], scale=-1.0)
                tt("dve", khall[:, h, :], t2[:], t4[:], ALU.mult, [t2, t4], [(khall, h)])
            yield
        assert NT * 4 <= KC
        for g in range(2):
            pt = ptr.next()
            for hh in range(4):
                h = 4 * g + hh
                for t in range(NT):
                    tr(pt[:, t * 4 + hh, :], khall[:, h, t * 128:(t + 1) * 128], ident[:], [(khall, h), ident], [pt])
            cp("act", khat[:, :, 4 * g:4 * g + 4, :], pt[:, 0:NT * 4, :].rearrange("p (t h) m -> p t h m", t=NT),
               [pt], [(khat, 4 * g + hh) for hh in range(4)])
            yield
        for p in range(NT):
            vk = [(vtm, p, 0), (vtm, p, 1)]
            pa = [pm.next(), pm.next()]
            for h in range(H):
                mm(pa[h // 4][:, (h % 4) * 128:(h % 4 + 1) * 128], ktT[:, h, p * 128:(p + 1) * 128],
                   qtT[:, h, p * 128:(p + 1) * 128], True, True, [(ktT, h), (qtT, h)], [pa[h // 4]])
            at = atr.next()
            for g in range(2):
                tt("dve", at[:, 4 * g:4 * g + 4, :], pa[g][:].rearrange("p (g t) -> p g t", g=4),
                   pmask[:].unsqueeze(1).to_broadcast([128, 4, 128]), ALU.mult, [pa[g], pmask], [(at, g)])
            yield
            pus = {}

            def u_mm(ci):
                r0 = 64 * ci
                pu = [pm.next(), pm.next()]
                for h in range(H):
                    mm(pu[h // 4][:, (h % 4) * 128:(h % 4 + 1) * 128], khat[r0:r0 + 64, p, h, :],
                       vtm[r0:r0 + 64, p, h * 128:(h + 1) * 128], True, True, vk + [(khat, h)], [pu[h // 4]])
                pus[ci] = pu

            def o_mm(ci):
                cur = cur_box[0]
                r0 = 64 * ci
                for h in range(H):
                    po = pd[h // 4]
                    c0 = (h % 4) * 128 + r0
                    mm(po[:, c0:c0 + 64], vtm[r0:r0 + 64, p, h * 128:(h + 1) * 128], at[r0:r0 + 64, h, r0:r0 + 64],
                       True, False, vk + [(at, h // 4)], [po])
                    mm(po[:, c0:c0 + 64], Sb[cur][:, h, :], qtT[:, h, p * 128 + r0:p * 128 + r0 + 64],
                       False, True, [(Sb[cur], h // 4), (qtT, h)], [po])

            def s_update(ci):
                cur = cur_box[0]
                gch = 2 * p + ci
                pu = pus[ci]
                nxt = 1 - cur
                for g in range(2):
                    hs = slice(4 * g, 4 * g + 4)
                    tt("dve", Sf[:, hs, :], Sf[:, hs, :], ebl[:, hs, gch, :].to_broadcast([128, 4, 128]), ALU.mult,
                       [(Sf, g)] + [(ebl, h) for h in range(4 * g, 4 * g + 4)], [(Sf, g)])
                    tt("dve", Sf[:, hs, :], Sf[:, hs, :], pu[g][:].rearrange("p (g t) -> p g t", g=4), ALU.add,
                       [(Sf, g), pu[g]], [(Sf, g)])
                    cp("act", Sb[nxt][:, hs, :], Sf[:, hs, :], [(Sf, g)], [(Sb[nxt], g)])
                cur_box[0] = nxt

            o_mm(0)
            yield
            u_mm(0)
            s_update(0)
            yield
            o_mm(1)
            yield
            u_mm(1)
            s_update(1)
            for g in range(2):
                cp("act", of32[:, 4 * g:4 * g + 4, p * 128:(p + 1) * 128], pd[g][:].rearrange("p (g t) -> p g t", g=4),
                   [pd[g]], [(of32, g, p)] + [("sigf", 4 * g + q) for q in range(4)])
            yield

    def R3(n):
        t0 = n * NB
        hT = hTs[n % 2]
        hTk = [(hT, t) for t in range(NT)]
        xrs = []
        for t in range(NT):
            xr = xrr.next()
            dma("sp", xr[:], x_d[t0 + t * 128:t0 + (t + 1) * 128, :], writes=[xr])
            xrs.append(xr)
        def onorm_tail(h, o16):
            ofk = [(of32, h // 4, p) for p in range(NT)]
            pss = pm.next()
            mm(pss[:, 0:NB], ones[:], o16[:], True, True, [ones, o16], [pss])
            ro = ror.next()
            act(ro[:], pss[:, 0:NB], AF.Ln, [pss, epst], [ro], scale=1.0 / 128, bias=epst[:, 0:1])
            act(ro[:], ro[:], AF.Exp, [ro], [ro], scale=-0.5)
            tt("dve", of32[:, h, :], of32[:, h, :], ro[:], ALU.mult, ofk + [ro], ofk)

        pend = None
        wgs = {}
        for h in range(H):
            if h % 4 == 0:
                wgs[h // 4] = wget()
            pg = pm.next()
            mm_fm(pg[:, 0:NB], pg, wgs[h // 4], h % 4, hT, NB, hTk)
            act(graw[:, h, :], pg[:, 0:NB], AF.Identity, [pg], [(graw, h)])
            ofk = [(of32, h // 4, p) for p in range(NT)]
            o16 = q16r.next()
            act(o16[:], of32[:, h, :], AF.Square, ofk, [o16])
            if pend is not None:
                onorm_tail(*pend)
            pend = (h, o16)
        onorm_tail(*pend)
        for h in range(H):
            ofk = [(of32, h // 4, p) for p in range(NT)]
            act(graw[:, h, :], graw[:, h, :], AF.Silu, [(graw, h)], [(graw, h)])
            stt("dve", og[:, h, :], of32[:, h, :], gng[:, h:h + 1], graw[:, h, :], ALU.mult, ALU.mult,
                ofk + [vecs, (graw, h)], [(og, h)])
        ogk = [(og, h) for h in range(H)]
        for uo in range(2):
            wm = wget()
            wr = wget()
            sms = []
            for j in range(4):
                pmr = pm.next()
                mm_fm(pmr[:, 0:NB], pmr, wm, j, hT, NB, hTk)
                sm = sgr.next()
                act(sm[:], pmr[:, 0:NB], AF.Sigmoid, [pmr], [sm])
                sms.append(sm)
            for j in range(4):
                m = 4 * uo + j
                sm = sms[j]
                py = pm.next()
                for k in range(KC):
                    mm(py[:, 0:NB], wr[:, k, j * 128:(j + 1) * 128], og[:, k, :], k == 0, k == KC - 1, [wr] + ogk, [py])
                tt("dve", sm[:], py[:, 0:NB], sm[:], ALU.mult, [py, sm], [sm])
                if n + 1 < NBLK:
                    ln_norm(m)
                tt("dve", merged[:, m, :], sm[:], ycg[:, m, :], ALU.add, [sm, (ycg, m)], [(merged, m)])
        mk = [(merged, m) for m in range(KC)]
        wo2 = [wget(), wget()]
        for t in range(NT):
            xr = xrs[t]
            for half in range(2):
                pp = pm.next()
                for k in range(KC):
                    mm(pp[:], merged[:, k, t * 128:(t + 1) * 128], wo2[half][:, k, :], k == 0, k == KC - 1,
                       mk + [wo2[half]], [pp])
                tt("dve", xr[:, half * 512:(half + 1) * 512], pp[:], xr[:, half * 512:(half + 1) * 512], ALU.add,
                   [pp, xr], [xr])
            ss = small.next()
            act(junk[:], xr[:], AF.Square, [xr], [junk, (ss, 0)], accum=ss[:, 0:1])
            act(ss[:, 1:2], ss[:, 0:1], AF.Ln, [(ss, 0), epst], [(ss, 1)], scale=1.0 / D, bias=epst[:, 0:1])
            act(ss[:, 2:3], ss[:, 1:2], AF.Exp, [(ss, 1)], [(ss, 2)], scale=-0.5)
            stt("dve", xr[:], xr[:], ss[:, 2:3], fgbc[:], ALU.mult, ALU.mult, [xr, (ss, 2), fgbc], [xr])
            finals.append(dma("sp", out_d[t0 + t * 128:t0 + (t + 1) * 128, :], xr[:], reads=[xr]))

    def interleave(ga, na, gb, nb):
        ia = ib = 0
        da = db = False
        while not (da and db):
            if not da and (db or ia * nb <= ib * na):
                try:
                    next(ga)
                    ia += 1
                except StopIteration:
                    da = True
            else:
                try:
                    next(gb)
                    ib += 1
                except StopIteration:
                    db = True

    def interleave_at(ga, gb, pos):
        for i, _ in enumerate(ga):
            if i in pos:
                next(gb, None)
        for _ in gb:
            pass

    for _ in C1(0):
        pass
    for c_ in range(KC):
        ln_norm(c_)
    for n in range(NBLK):
        C2R1(n)
        if n + 1 < NBLK:
            gc = C1(n + 1)
            next(gc)
            interleave_at(R2(n), gc, (0, 1, 2, 3, 5, 6, 7, 8))
        else:
            for _ in R2(n):
                pass
        R3(n)
    counts = S.emit(final_waits=finals)
    return nc, counts


SEQ_FULL = 4096
NB_FULL = 256
_CACHE = {}


def kernel(**inputs):
    x = np.asarray(inputs["x"], np.float32)
    ncore = x.shape[0]
    lay = host_layout(inputs)
    if "nc" not in _CACHE:
        _CACHE["nc"] = build_program(SEQ_FULL, NB_FULL)[0]
    nc = _CACHE["nc"]
    in_maps = []
    for c in range(ncore):
        im = dict(lay)
        im["x"] = np.ascontiguousarray(x[c])
        in_maps.append(im)
    res = run_bass_kernel_spmd(nc, in_maps, core_ids=list(range(ncore)))
    return np.stack([np.asarray(r["out"], np.float32) for r in res.results], axis=0)
```
